# Optimizing a Trainium2 kernel written in Bass

```python
import math, functools
import jax, jax.numpy as jnp
from jax import lax
import numpy as np

D_MODEL = 1024
BATCH = 1
SEQ = 16384
DEPTH = 4

D_FF = ((8 * D_MODEL // 3 + 127) // 128) * 128
FFN_HALF = 0.5
N_SUB = 3

ML_HEADS = 4
ML_QK = D_MODEL // 2
ML_V = D_MODEL
ML_DK = ML_QK // ML_HEADS
ML_DV = ML_V // ML_HEADS
ML_CHUNK = 64
ML_IN = 2 * ML_QK + 2 * ML_V + 2 * ML_HEADS

MB_EXPAND = 2
MB_DI = MB_EXPAND * D_MODEL
MB_HEADDIM = 64
MB_HEADS = MB_DI // MB_HEADDIM
MB_GROUPS = 4
MB_HPG = MB_HEADS // MB_GROUPS
MB_STATE = 128
MB_CONV = 4
MB_CHUNK = 128
MB_CONV_DIM = MB_DI + 2 * MB_GROUPS * MB_STATE
MB_IN = 2 * MB_DI + 2 * MB_GROUPS * MB_STATE + MB_HEADS

N_MLSTM_LAYERS = (DEPTH + 1) // 2
N_MAMBA_LAYERS = DEPTH // 2
EPS = 1e-6

kernel_name = "hybrid_mlstm_mamba2_macaron_adaln"


def rms_norm(x, g):
    xf = x.astype(jnp.float32)
    y = xf * lax.rsqrt(jnp.mean(xf * xf, axis=-1, keepdims=True) + EPS)
    return (y * g.astype(jnp.float32)).astype(x.dtype)


def _to_chunks(t, L):
    B, S = t.shape[:2]
    return jnp.moveaxis(t.reshape(B, S // L, L, *t.shape[2:]), 1, 0)


def _mlstm_chunk(carry, xs):
    C, n, m = carry
    q, k, v, ig, lf = xs
    L = q.shape[2]
    causal = jnp.tril(jnp.ones((L, L), dtype=bool))
    b = jnp.cumsum(lf, axis=-1)
    log_d = jnp.where(causal, b[..., :, None] - b[..., None, :] + ig[..., None, :], -jnp.inf)
    m_inter = b + m[..., None]
    m_t = jnp.maximum(jnp.max(log_d, axis=-1), m_inter)
    s = jnp.einsum('bhtd,bhsd->bhts', q, k) * jnp.exp(log_d - m_t[..., None])
    inter = jnp.exp(m_inter - m_t)
    num = jnp.einsum('bhts,bhsv->bhtv', s, v) + inter[..., None] * jnp.einsum('bhtd,bhdv->bhtv', q, C)
    den = jnp.sum(s, axis=-1) + inter * jnp.einsum('bhtd,bhd->bht', q, n)
    h = num / jnp.maximum(jnp.abs(den), jnp.exp(-m_t))[..., None]
    b_last = b[..., -1]
    log_w = b_last[..., None] - b + ig
    m_new = jnp.maximum(b_last + m, jnp.max(log_w, axis=-1))
    w = jnp.exp(log_w - m_new[..., None])
    decay = jnp.exp(b_last + m - m_new)
    C_new = decay[..., None, None] * C + jnp.einsum('bhs,bhsd,bhsv->bhdv', w, k, v)
    n_new = decay[..., None] * n + jnp.einsum('bhs,bhsd->bhd', w, k)
    return (C_new, n_new, m_new), h


def mlstm_mixer(h, w_in, b_gate, norm_w, w_out):
    B, S, _ = h.shape
    proj = h @ w_in
    q, k, v, o, gates = jnp.split(proj, [ML_QK, 2 * ML_QK, 2 * ML_QK + ML_V, 2 * ML_QK + 2 * ML_V], axis=-1)
    f32 = jnp.float32
    q = q.reshape(B, S, ML_HEADS, ML_DK).astype(f32)
    k = k.reshape(B, S, ML_HEADS, ML_DK).astype(f32) * (ML_DK ** -0.5)
    v = v.reshape(B, S, ML_HEADS, ML_DV).astype(f32)
    gates = gates.astype(f32) + b_gate.astype(f32)
    ig, fg = jnp.split(gates, 2, axis=-1)
    lf = jax.nn.log_sigmoid(fg)
    xs = tuple(jnp.swapaxes(_to_chunks(t, ML_CHUNK), 2, 3) for t in (q, k, v, ig, lf))
    carry0 = (jnp.zeros((B, ML_HEADS, ML_DK, ML_DV), f32),
              jnp.zeros((B, ML_HEADS, ML_DK), f32),
              jnp.zeros((B, ML_HEADS), f32))
    _, hs = lax.scan(_mlstm_chunk, carry0, xs)
    hs = jnp.moveaxis(jnp.swapaxes(hs, 2, 3), 0, 1).reshape(B, S, ML_HEADS, ML_DV)
    hs = hs * lax.rsqrt(jnp.mean(hs * hs, axis=-1, keepdims=True) + EPS)
    hs = hs.reshape(B, S, ML_V) * norm_w.astype(f32) * jax.nn.sigmoid(o.astype(f32))
    return hs.astype(h.dtype) @ w_out


def _ssd_chunk(state, xs, A):
    x, dt, Bc, Cc = xs
    Q = x.shape[1]
    causal = jnp.tril(jnp.ones((Q, Q), dtype=bool))
    a = jnp.cumsum(dt * A, axis=1)
    seg = jnp.where(causal[None, :, :, None, None], a[:, :, None] - a[:, None, :], -jnp.inf)
    w = jnp.einsum('btgn,bsgn->btsg', Cc, Bc)[..., None] * jnp.exp(seg) * dt[:, None]
    y = (jnp.einsum('btsgh,bsghp->btghp', w, x)
         + jnp.exp(a)[..., None] * jnp.einsum('btgn,bghpn->btghp', Cc, state))
    a_last = a[:, -1]
    ws = jnp.exp(a_last[:, None] - a) * dt
    state_new = (jnp.exp(a_last)[..., None, None] * state
                 + jnp.einsum('bsgn,bsgh,bsghp->bghpn', Bc, ws, x))
    return state_new, y


def mamba2_mixer(h, w_in, conv_w, conv_b, dt_bias, A_log, D_skip, norm_w, w_out):
    B, S, _ = h.shape
    f32 = jnp.float32
    proj = h @ w_in
    z, xbc, dt = jnp.split(proj, [MB_DI, MB_DI + MB_CONV_DIM], axis=-1)
    xbc = lax.conv_general_dilated(xbc, conv_w[:, None, :].astype(xbc.dtype), window_strides=(1,),
                                   padding=[(MB_CONV - 1, 0)], dimension_numbers=('NWC', 'WIO', 'NWC'),
                                   feature_group_count=MB_CONV_DIM)
    xbc = jax.nn.silu((xbc + conv_b).astype(f32))
    xs, Bm, Cm = jnp.split(xbc, [MB_DI, MB_DI + MB_GROUPS * MB_STATE], axis=-1)
    xs = xs.reshape(B, S, MB_GROUPS, MB_HPG, MB_HEADDIM)
    Bm = Bm.reshape(B, S, MB_GROUPS, MB_STATE)
    Cm = Cm.reshape(B, S, MB_GROUPS, MB_STATE)
    dt = jax.nn.softplus(dt.astype(f32) + dt_bias.astype(f32)).reshape(B, S, MB_GROUPS, MB_HPG)
    A = -jnp.exp(A_log.astype(f32)).reshape(MB_GROUPS, MB_HPG)
    chunks = tuple(_to_chunks(t, MB_CHUNK) for t in (xs, dt, Bm, Cm))
    state0 = jnp.zeros((B, MB_GROUPS, MB_HPG, MB_HEADDIM, MB_STATE), f32)
    _, ys = lax.scan(functools.partial(_ssd_chunk, A=A), state0, chunks)
    y = jnp.moveaxis(ys, 0, 1).reshape(B, S, MB_GROUPS, MB_HPG, MB_HEADDIM)
    y = y + D_skip.astype(f32).reshape(MB_GROUPS, MB_HPG)[..., None] * xs
    y = y.reshape(B, S, MB_DI) * jax.nn.silu(z.astype(f32))
    y = y.reshape(B, S, MB_GROUPS, MB_DI // MB_GROUPS)
    y = y * lax.rsqrt(jnp.mean(y * y, axis=-1, keepdims=True) + EPS)
    y = y.reshape(B, S, MB_DI) * norm_w.astype(f32)
    return y.astype(h.dtype) @ w_out


def swiglu(h, w1, w3, w2):
    return (jax.nn.silu(h @ w1) * (h @ w3)) @ w2


def sublayer(x, fn, g_pre, g_post, shift, scale, gate, weight):
    hmod = rms_norm(x, g_pre) * (1 + scale[:, None, :]) + shift[:, None, :]
    y = rms_norm(fn(hmod), g_post)
    return x + weight * gate[:, None, :] * y


def setup_inputs(seed: int = 0) -> dict:
    key = jax.random.key(seed)
    ks = jax.random.split(key, 24)
    D = D_MODEL
    nrm = lambda k, shape, fan_in: jax.random.normal(k, shape, jnp.float32) * (fan_in ** -0.5)
    u_dt = jax.random.uniform(ks[17], (N_MAMBA_LAYERS, MB_HEADS), jnp.float32)
    dt0 = jnp.exp(u_dt * (math.log(0.1) - math.log(1e-3)) + math.log(1e-3))
    ig_b = 0.5 * jax.random.normal(ks[12], (N_MLSTM_LAYERS, ML_HEADS), jnp.float32)
    fg_b = 3.0 + 3.0 * jax.random.uniform(ks[13], (N_MLSTM_LAYERS, ML_HEADS), jnp.float32)
    return {
        "x": jax.random.normal(ks[0], (BATCH, SEQ, D), jnp.float32),
        "c": jax.random.normal(ks[1], (BATCH, D), jnp.float32),
        "ada_w": 0.5 * nrm(ks[2], (DEPTH, D, 3 * N_SUB * D), D),
        "ada_b": 0.02 * jax.random.normal(ks[3], (DEPTH, 3 * N_SUB * D), jnp.float32),
        "norm_pre": 1.0 + 0.1 * jax.random.normal(ks[4], (DEPTH, N_SUB, D), jnp.float32),
        "norm_post": 1.0 + 0.1 * jax.random.normal(ks[5], (DEPTH, N_SUB, D), jnp.float32),
        "ffn_w1": nrm(ks[6], (DEPTH, 2, D, D_FF), D),
        "ffn_w3": nrm(ks[7], (DEPTH, 2, D, D_FF), D),
        "ffn_w2": nrm(ks[8], (DEPTH, 2, D_FF, D), D_FF),
        "ml_w_in": nrm(ks[9], (N_MLSTM_LAYERS, D, ML_IN), D),
        "ml_b_gate": jnp.concatenate([ig_b, fg_b], axis=-1),
        "ml_norm_w": 1.0 + 0.1 * jax.random.normal(ks[10], (N_MLSTM_LAYERS, ML_V), jnp.float32),
        "ml_w_out": nrm(ks[11], (N_MLSTM_LAYERS, ML_V, D), ML_V),
        "mb_w_in": nrm(ks[14], (N_MAMBA_LAYERS, D, MB_IN), D),
        "mb_conv_w": nrm(ks[15], (N_MAMBA_LAYERS, MB_CONV, MB_CONV_DIM), MB_CONV),
        "mb_conv_b": 0.02 * jax.random.normal(ks[16], (N_MAMBA_LAYERS, MB_CONV_DIM), jnp.float32),
        "mb_dt_bias": dt0 + jnp.log(-jnp.expm1(-dt0)),
        "mb_A_log": jnp.log(jax.random.uniform(ks[18], (N_MAMBA_LAYERS, MB_HEADS), jnp.float32, 1.0, 16.0)),
        "mb_D": 1.0 + 0.1 * jax.random.normal(ks[19], (N_MAMBA_LAYERS, MB_HEADS), jnp.float32),
        "mb_norm_w": 1.0 + 0.1 * jax.random.normal(ks[20], (N_MAMBA_LAYERS, MB_DI), jnp.float32),
        "mb_w_out": nrm(ks[21], (N_MAMBA_LAYERS, MB_DI, D), MB_DI),
    }


def reference(x, c, ada_w, ada_b, norm_pre, norm_post, ffn_w1, ffn_w3, ffn_w2,
              ml_w_in, ml_b_gate, ml_norm_w, ml_w_out,
              mb_w_in, mb_conv_w, mb_conv_b, mb_dt_bias, mb_A_log, mb_D, mb_norm_w, mb_w_out):
    B = x.shape[0]
    c_act = jax.nn.silu(c)
    for i in range(DEPTH):
        mod = (c_act @ ada_w[i] + ada_b[i]).reshape(B, N_SUB, 3, D_MODEL)
        j = i // 2
        if i % 2 == 0:
            mixer = lambda h, j=j: mlstm_mixer(h, ml_w_in[j], ml_b_gate[j], ml_norm_w[j], ml_w_out[j])
        else:
            mixer = lambda h, j=j: mamba2_mixer(h, mb_w_in[j], mb_conv_w[j], mb_conv_b[j], mb_dt_bias[j],
                                                mb_A_log[j], mb_D[j], mb_norm_w[j], mb_w_out[j])
        fns = (lambda h, i=i: swiglu(h, ffn_w1[i, 0], ffn_w3[i, 0], ffn_w2[i, 0]),
               mixer,
               lambda h, i=i: swiglu(h, ffn_w1[i, 1], ffn_w3[i, 1], ffn_w2[i, 1]))
        weights = (FFN_HALF, 1.0, FFN_HALF)
        for s in range(N_SUB):
            x = sublayer(x, fns[s], norm_pre[i, s], norm_post[i, s],
                         mod[:, s, 0], mod[:, s, 1], mod[:, s, 2], weights[s])
    return x
```

```python
import contextlib
import math
import numpy as np
import concourse.bass as bass
import concourse.mybir as mybir
from concourse.bass_utils import run_bass_kernel_spmd

F32 = mybir.dt.float32
BF16 = mybir.dt.bfloat16
AF = mybir.ActivationFunctionType
ALU = mybir.AluOpType

import os
NCORES = 8
TPC = int(os.environ.get("K_TPC", "2048"))
SEQ = TPC * NCORES
NT = TPC // 128
NG = NT // 4
D = 1024
DFF = 2816
NFC = DFF // 128
EPS = 1e-6
SAME_ENGINE_SYNC = True
DMA_RING = 8


class Prog:
    ENGS = ("pe", "act", "dve", "pool", "sp")

    def __init__(self, nc):
        self.nc = nc
        self.ops = []
        self.last_w = {}
        self.readers = {}

    def op(self, eng, fn, reads=(), writes=(), dma=False):
        idx = len(self.ops)
        deps = set()
        for k in reads:
            w = self.last_w.get(k)
            if w is not None:
                deps.add(w)
        for k in writes:
            w = self.last_w.get(k)
            if w is not None:
                deps.add(w)
            for r in self.readers.get(k, ()):
                deps.add(r)
        deps.discard(idx)
        for k in reads:
            self.readers.setdefault(k, []).append(idx)
        for k in writes:
            self.last_w[k] = idx
            self.readers[k] = []
        self.ops.append(dict(eng=eng, fn=fn, deps=deps, dma=dma))
        return idx

    def emit(self, ctx, final_wait_ops=()):
        nc = self.nc
        ops = self.ops
        n = len(ops)
        need_sig = [False] * n
        for i, o in enumerate(ops):
            for d in o["deps"]:
                p = ops[d]
                if p["dma"]:
                    continue
                if p["eng"] == o["eng"] and not o["dma"]:
                    if p["eng"] == "pe" or not SAME_ENGINE_SYNC:
                        continue
                need_sig[d] = True
        for d in final_wait_ops:
            if not ops[d]["dma"]:
                need_sig[d] = True
        last = {}
        for i, o in enumerate(ops):
            if not o["dma"]:
                last[o["eng"]] = i
        for i in last.values():
            need_sig[i] = True
        cnt, dcnt = ctx.cnt, ctx.dcnt
        for i, o in enumerate(ops):
            e = o["eng"]
            if o["dma"]:
                k = dcnt[e]
                dcnt[e] += 1
                o["sem"] = ("d", e, k % DMA_RING)
                o["val"] = 16 * (k // DMA_RING + 1)
                o["dma_k"] = k
            elif need_sig[i]:
                cnt[e] += 1
                o["sem"] = ("e", e)
                o["val"] = cnt[e]
            else:
                o["sem"] = None
        per_eng = {e: [i for i, o in enumerate(ops) if o["eng"] == e] for e in self.ENGS}
        self.stats = {e: len(per_eng[e]) for e in self.ENGS}
        sems = ctx.get_sems(nc)
        targets = []
        for e in self.ENGS:
            if cnt[e] > 0:
                targets.append((("e", e), cnt[e]))
            for r in range(DMA_RING):
                if dcnt[e] > r:
                    targets.append((("d", e, r), 16 * ((dcnt[e] - r + DMA_RING - 1) // DMA_RING)))
        self.sched = {e: [] for e in self.ENGS}
        with nc.Block() as block:

            def replay(e, eng):
                waited = ctx.waited[e]
                cur = []

                def wait(semkey, val):
                    if waited.get(semkey, 0) >= val:
                        return
                    eng.wait_ge(sems[semkey], val)
                    waited[semkey] = val
                    cur.append((semkey, val))

                for i in per_eng[e]:
                    o = ops[i]
                    for d in sorted(o["deps"]):
                        p = ops[d]
                        if p["eng"] == e and not p["dma"] and not o["dma"]:
                            if e == "pe" or not SAME_ENGINE_SYNC:
                                continue
                        wait(p["sem"], p["val"])
                    if o["dma"] and o["dma_k"] >= DMA_RING:
                        wait(o["sem"], o["val"] - 16)
                    ins = o["fn"](eng)
                    if o["sem"] is not None:
                        ins.then_inc(sems[o["sem"]], 16 if o["dma"] else 1)
                    self.sched[e].append((i, list(cur), o["sem"], 16 if o["dma"] else 1))
                    del cur[:]
                for semkey, val in targets:
                    wait(semkey, val)
                self.sched[e].append((-1, list(cur), None, 0))

            names = {"pe": "tensor", "act": "scalar", "dve": "vector", "pool": "gpsimd", "sp": "sync"}
            for e in self.ENGS:
                getattr(block, names[e])(lambda eng, e=e: replay(e, eng))


class Ctx:
    def __init__(self):
        self.cnt = {e: 0 for e in Prog.ENGS}
        self.dcnt = {e: 0 for e in Prog.ENGS}
        self.waited = {e: {} for e in Prog.ENGS}
        self.simval = {}
        self.sems = None
        self.es = contextlib.ExitStack()

    def get_sems(self, nc):
        if self.sems is None:
            self.sems = {}
            for e in Prog.ENGS:
                self.sems[("e", e)] = self.es.enter_context(nc.semaphore("s_" + e))
            for e in ("sp", "pool"):
                for r in range(DMA_RING):
                    self.sems[("d", e, r)] = self.es.enter_context(nc.semaphore("d_%s_%d" % (e, r)))
        return self.sems


def simulate(P, val):
    pos = {e: 0 for e in P.ENGS}
    prog = True
    while prog:
        prog = False
        for e in P.ENGS:
            while pos[e] < len(P.sched[e]):
                i, waits, sem, inc = P.sched[e][pos[e]]
                if all(val.get(k, 0) >= v for k, v in waits):
                    if sem is not None:
                        val[sem] = val.get(sem, 0) + inc
                    pos[e] += 1
                    prog = True
                else:
                    break
    bad = {e: P.sched[e][pos[e]] for e in P.ENGS if pos[e] < len(P.sched[e])}
    return bad


class Ring:
    def __init__(self, lb, name, shape, dt, n):
        self.t = [lb.sb("%s%d" % (name, i), shape, dt) for i in range(n)]
        self.k = ["%s%d" % (name, i) for i in range(n)]
        self.i = 0

    def next(self):
        i = self.i
        self.i = (i + 1) % len(self.t)
        return self.t[i], self.k[i]


class Fused:
    def __init__(self):
        self.nc = bass.Bass("TRN2", target_bir_lowering=False)
        self.es = contextlib.ExitStack()
        self.ctx = Ctx()
        self.banks = [self.es.enter_context(self.nc.psum_tensor("bank%d" % i, [128, 512], F32)) for i in range(8)]
        self.bi = 0
        self.declared = {}
        self.in_names = []

    def dram_in(self, name, shape, dt=F32):
        if name not in self.declared:
            self.declared[name] = self.nc.dram_tensor(name, list(shape), dt, kind="ExternalInput").ap()
            self.in_names.append(name)
        return self.declared[name]

    def close(self):
        self.ctx.es.close()
        self.es.close()
        return self.nc


SHARED_INPUTS = ("ident", "tri", "slow", "c_l", "pm", "sel_rep4", "sel_rep32", "tail_sel")


class LB:
    def __init__(self, parent=None, tag="", io=None):
        self.fused = parent is not None
        self.parent = parent if parent is not None else Fused()
        self.tag = tag
        self.io = io or {}
        self.nc = self.parent.nc
        self.es = contextlib.ExitStack()
        self.P = Prog(self.nc)
        self.out_ops = []
        self.banks = self.parent.banks

    def inp(self, name, shape, dt=F32):
        if name in self.io:
            return self.io[name]
        full = name if name in SHARED_INPUTS else self.tag + name
        return self.parent.dram_in(full, shape, dt)

    def out(self, name, shape, dt=F32):
        if name in self.io:
            return self.io[name]
        return self.nc.dram_tensor(self.tag + name, list(shape), dt, kind="ExternalOutput").ap()

    def sb(self, name, shape, dt=F32):
        return self.es.enter_context(self.nc.sbuf_tensor(self.tag + name, list(shape), dt))

    def psum(self):
        i = self.parent.bi
        self.parent.bi = (i + 1) % 8
        return self.banks[i], ("ps", i)

    def collect(self, src, dst, r=(), w=()):
        o = self.P.op("pool", lambda e: e.collective_compute(
            "AllGather", ALU.bypass, replica_groups=[list(range(NCORES))], ins=[src.opt()], outs=[dst.opt()]), r, w)
        self.out_ops.append(o)

    def finish(self):
        ctx = self.parent.ctx
        self.P.emit(ctx, self.out_ops)
        bad = simulate(self.P, ctx.simval)
        print("PROG", self.tag, self.P.stats, "deadlock" if bad else "sim-ok", bad if bad else "")
        assert not bad
        self.es.close()
        if not self.fused:
            return self.parent.close()
        return None

    def MM(self, out, lhsT, rhs, start=True, stop=True, r=(), w=()):
        self.P.op("pe", lambda e: e.matmul(out, lhsT, rhs, start=start, stop=stop), r, w)

    def TR(self, out, in_, r=(), w=()):
        idb = self.identb
        self.P.op("pe", lambda e: e.transpose(out, in_, idb[:]), list(r) + ["identb"], w)

    def ACT(self, out, in_, func, r=(), w=(), **kw):
        self.P.op("act", lambda e: e.activation(out=out, in_=in_, func=func, **kw), r, w)

    def TT(self, out, in0, in1, op, r=(), w=(), eng="dve"):
        self.P.op(eng, lambda e: e.tensor_tensor(out=out, in0=in0, in1=in1, op=op), r, w)

    def STT(self, out, in0, scalar, in1, op0, op1, r=(), w=(), accum_out=None, eng="dve"):
        if accum_out is None:
            self.P.op(eng, lambda e: e.scalar_tensor_tensor(out=out, in0=in0, scalar=scalar, in1=in1, op0=op0, op1=op1), r, w)
        else:
            self.P.op(eng, lambda e: e.scalar_tensor_tensor(out=out, in0=in0, scalar=scalar, in1=in1, op0=op0, op1=op1,
                                                            accum_out=accum_out), r, w)

    def TSM(self, out, in0, s1, r=(), w=(), eng="dve"):
        self.P.op(eng, lambda e: e.tensor_scalar_mul(out=out, in0=in0, scalar1=s1), r, w)

    def TSA(self, out, in0, s1, r=(), w=(), eng="dve"):
        self.P.op(eng, lambda e: e.tensor_scalar_add(out=out, in0=in0, scalar1=s1), r, w)

    def RECIP(self, out, in_, r=(), w=()):
        self.P.op("dve", lambda e: e.reciprocal(out=out, in_=in_), r, w)

    def CP(self, out, in_, r=(), w=(), eng="dve"):
        self.P.op(eng, lambda e: e.tensor_copy(out=out, in_=in_), r, w)

    def MS(self, ap, val, w=(), eng="dve"):
        self.P.op(eng, lambda e: e.memset(ap, val), (), w)

    def DMA(self, out, in_, r=(), w=(), q="sp"):
        return self.P.op(q, lambda e: e.dma_start(out=out, in_=in_), r, w, dma=True)

    def consts(self):
        ident = self.inp("ident", [128, 128])
        tri = self.inp("tri", [128, 128])
        self.identf = self.sb("identf", [128, 128])
        self.identb = self.sb("identb", [128, 128], BF16)
        self.trif = self.sb("trif", [128, 128])
        self.onesf = self.sb("onesf", [128, 128])
        self.DMA(self.identf[:], ident, w=["identf"])
        self.DMA(self.identb[:], ident, w=["identb"], q="pool")
        self.DMA(self.trif[:], tri, w=["trif"])
        self.MS(self.onesf[:], 1.0, w=["onesf"])

    def xgroup(self, gs=4, rows=TPC):
        self.gs = gs
        self.x_in = self.inp("x", [rows, D])
        self.xg = self.sb("xg", [128, gs, D])

    def load_xg(self, g):
        n = self.gs * 128
        self.DMA(self.xg[:], self.x_in[g * n:(g + 1) * n, :].rearrange("(t p) d -> p t d", p=128),
                 w=[("x", i) for i in range(self.gs)])

    def store_xg(self, g, x_out):
        n = self.gs * 128
        o = self.DMA(x_out[g * n:(g + 1) * n, :].rearrange("(t p) d -> p t d", p=128), self.xg[:],
                     r=[("x", i) for i in range(self.gs)])
        self.out_ops.append(o)

    def norm_scratch(self):
        self.hmf = self.sb("hmf", [128, D])
        self.hmb = Ring(self, "hmb", [128, D], BF16, 2)
        self.junk = self.sb("junk", [128, D], BF16)
        self.ssr = Ring(self, "ss", [128, 4], F32, 2)

    def mod(self, weight, scratch_f32):
        c_in = self.inp("c_l", [128, 8])
        adaw = self.inp("ada_w", [D, 3 * D])
        adab = self.inp("ada_b", [1, 3 * D])
        gpre = self.inp("g_pre", [1, D])
        gpost = self.inp("g_post", [1, D])
        cs = self.sb("cs", [128, 8])
        A = self.sb("modA", [128, D])
        SH = self.sb("modSH", [128, D])
        G = self.sb("modG", [128, D])
        st, stk = self.hmf, "hmf"
        scr, scrk = scratch_f32
        self.DMA(cs[:], c_in, w=["cs"])
        self.ACT(cs[:], cs[:], AF.Silu, r=["cs"], w=["cs"])
        dsts = ((SH, "SH"), (A, "A"), (G, "G"))
        for j, (dst, key) in enumerate(dsts):
            self.DMA(dst[:], adab[0, j * D:(j + 1) * D].partition_broadcast(128), w=[key])
        for j, (dst, key) in enumerate(dsts):
            for b in range(2):
                col = j * D + b * 512
                wb = scr.rearrange("p (k n) -> p k n", k=8)
                self.DMA(wb, adaw[:, col:col + 512].rearrange("(kc p) n -> p kc n", p=128), w=scrk)
                acc = self.junkf
                self.TSM(acc[:], wb[:, 0, :], cs[:, 0:1], r=list(scrk) + ["cs"], w=["junkf"])
                for kc in range(1, 8):
                    self.STT(acc[:], wb[:, kc, :], cs[:, kc:kc + 1], acc[:], ALU.mult, ALU.add,
                             r=list(scrk) + ["cs", "junkf"], w=["junkf"])
                ps, pk = self.psum()
                self.MM(ps[:, :], self.onesf[:], acc[:], r=["onesf", "junkf"], w=[pk])
                sl = dst[:, b * 512:(b + 1) * 512]
                self.TT(sl, ps[:, :], sl, ALU.add, r=[pk, key], w=[key])
        self.DMA(st[:], gpre[0, :].partition_broadcast(128), w=[stk])
        self.STT(A[:], A[:], 1.0, st[:], ALU.add, ALU.mult, r=["A", stk], w=["A"])
        self.DMA(st[:], gpost[0, :].partition_broadcast(128), w=[stk])
        self.STT(G[:], G[:], float(weight), st[:], ALU.mult, ALU.mult, r=["G", stk], w=["G"])
        self.A, self.SH, self.G = A, SH, G

    def prenorm_T(self, tt, hmT, hmTk, col):
        x_t = self.xg[:, tt, :]
        xk = ("x", tt)
        ss, ssk = self.ssr.next()
        hmb, hmbk = self.hmb.next()
        self.STT(self.junk[:], x_t, 1.0, x_t, ALU.mult, ALU.mult, r=[xk], w=["junk", ssk], accum_out=ss[:, 0:1])
        self.ACT(ss[:, 1:2], ss[:, 0:1], AF.Sqrt, r=[ssk], w=[ssk], scale=1.0 / D, bias=EPS)
        self.RECIP(ss[:, 2:3], ss[:, 1:2], r=[ssk], w=[ssk])
        self.STT(self.hmf[:], x_t, ss[:, 2:3], self.A[:], ALU.mult, ALU.mult, r=[xk, ssk, "A"], w=["hmf"])
        self.TT(hmb[:], self.hmf[:], self.SH[:], ALU.add, r=["hmf", "SH"], w=[hmbk])
        ps, pk = self.psum()
        psb = ps[:].bitcast(BF16)
        for kc in range(8):
            self.TR(psb[:, kc * 128:(kc + 1) * 128], hmb[:, kc * 128:(kc + 1) * 128], r=[hmbk], w=[pk])
        self.ACT(hmT[:, :, col * 128:(col + 1) * 128], psb.rearrange("p (a b) -> p a b", a=8), AF.Copy,
                 r=[pk], w=[hmTk])

    def epi_alloc(self):
        self.yb = self.sb("yb", [128, 4, D])
        self.es2 = self.sb("es2", [128, 4, 4])
        self.junkf = self.sb("junkf", [128, 512])
        self.rawr = Ring(self, "raw", [128, 512], F32, 2)

    def epi_half(self, tt, h, ps, pk):
        raw, rk = self.rawr.next()
        self.ACT(raw[:], ps[:, :], AF.Copy, r=[pk], w=[rk])
        self.STT(self.junkf[:], raw[:], 1.0, raw[:], ALU.mult, ALU.mult, r=[rk], w=["junkf", ("es2", tt)],
                 accum_out=self.es2[:, tt, h:h + 1])
        self.TT(self.yb[:, tt, h * 512:(h + 1) * 512], ps[:, :], self.G[:, h * 512:(h + 1) * 512], ALU.mult,
                r=[pk, "G"], w=[("yb", tt)])

    def epi_final(self, tt):
        e = self.es2
        k = ("es2", tt)
        self.TT(e[:, tt, 2:3], e[:, tt, 0:1], e[:, tt, 1:2], ALU.add, r=[k], w=[k])
        self.ACT(e[:, tt, 3:4], e[:, tt, 2:3], AF.Sqrt, r=[k], w=[k], scale=1.0 / D, bias=EPS)
        self.RECIP(e[:, tt, 2:3], e[:, tt, 3:4], r=[k], w=[k])
        self.STT(self.xg[:, tt, :], self.yb[:, tt, :], e[:, tt, 2:3], self.xg[:, tt, :], ALU.mult, ALU.add,
                 r=[k, ("yb", tt), ("x", tt)], w=[("x", tt)])


def build_ffn(lb=None):
    lb = lb or LB()
    lb.consts()
    lb.xgroup()
    lb.norm_scratch()
    lb.epi_alloc()
    x_out = lb.out("x_out", [TPC, D])
    w13 = lb.inp("w13", [NFC, 128, 2048])
    w2 = lb.inp("w2", [DFF, D])
    gT = lb.sb("gT", [128, NFC, 512], BF16)
    hmT = lb.sb("hmT", [128, 8, 512], BF16)
    w13r = Ring(lb, "w13b", [128, 2048], BF16, 3)
    w2r = Ring(lb, "w2b", [128, 512], BF16, 4)
    silr = Ring(lb, "sil", [128, 512], F32, 2)
    lb.mod(0.5, (lb.yb[:].rearrange("p a b -> p (a b)"), [("yb", i) for i in range(4)]))
    for g in range(NG):
        lb.load_xg(g)
        for tt in range(4):
            lb.prenorm_T(tt, hmT, "hmT", tt)
        for c in range(NFC):
            wb, wk = w13r.next()
            lb.DMA(wb[:], w13[c], w=[wk], q="pool")
            ps1, k1 = lb.psum()
            ps3, k3 = lb.psum()
            for kc in range(8):
                lb.MM(ps1[:, :], wb[:, kc * 128:(kc + 1) * 128], hmT[:, kc, :], start=(kc == 0), stop=(kc == 7),
                      r=[wk, "hmT"], w=[k1])
            for kc in range(8):
                lb.MM(ps3[:, :], wb[:, 1024 + kc * 128:1024 + (kc + 1) * 128], hmT[:, kc, :], start=(kc == 0),
                      stop=(kc == 7), r=[wk, "hmT"], w=[k3])
            sl, sk = silr.next()
            lb.ACT(sl[:], ps1[:, :], AF.Silu, r=[k1], w=[sk])
            lb.TT(gT[:, c, :], sl[:], ps3[:, :], ALU.mult, r=[sk, k3], w=[("gT", c)])
        for h in range(2):
            accs = [lb.psum() for _ in range(4)]
            for c in range(NFC):
                w2t, w2k = w2r.next()
                lb.DMA(w2t[:], w2[c * 128:(c + 1) * 128, h * 512:(h + 1) * 512], w=[w2k], q="pool")
                for tt in range(4):
                    lb.MM(accs[tt][0][:, :], gT[:, c, tt * 128:(tt + 1) * 128], w2t[:], start=(c == 0),
                          stop=(c == NFC - 1), r=[("gT", c), w2k], w=[accs[tt][1]])
            for tt in range(4):
                lb.epi_half(tt, h, accs[tt][0], accs[tt][1])
        for tt in range(4):
            lb.epi_final(tt)
        lb.store_xg(g, x_out)
    return lb.finish()


ML_DK = 128


def build_ml(full, lb=None):
    lb = lb or LB()
    lb.consts()
    lb.xgroup()
    lb.norm_scratch()
    lb.junkf = lb.sb("junkf", [128, 512])
    ncol = 3080 if full else 1544
    if full:
        cq, ck, cv, co, cg = 0, 512, 1024, 2048, 3072
    else:
        ck, cv, cg = 0, 512, 1536
    w_in = lb.inp("w_in", [D, ncol])
    bgate = lb.inp("b_gate", [1, 8])
    win = lb.sb("win", [128, 8, ncol], BF16)
    for kc in range(8):
        lb.DMA(win[:, kc, :], w_in[kc * 128:(kc + 1) * 128, :], w=[("win", kc)], q="pool")
    wink = [("win", kc) for kc in range(8)]
    bg = lb.sb("bg", [128, 8])
    lb.DMA(bg[:], bgate[0, :].partition_broadcast(128), w=["bg"])
    C = lb.sb("C", [128, 4, 257])
    hmTr = Ring(lb, "hmT", [128, 8, 128], BF16, 2)
    gtr = Ring(lb, "gt", [128, 24], F32, 2)
    kpr = Ring(lb, "kp", [128, 4, 128], BF16, 2)
    vextr = Ring(lb, "vext", [128, 4, 257], BF16, 2)
    for i in range(2):
        lb.MS(vextr.t[i][:, :, 256:257], 1.0, w=[vextr.k[i]])
    gacc = lb.sb("gacc", [128, 4])
    lb.MS(gacc[:], 0.0, w=["gacc"])
    lb.MS(C[:], 0.0, w=[("C", h) for h in range(4)])
    if full:
        lb.rawr = Ring(lb, "raw", [128, 512], F32, 2)
        lb.yb = lb.sb("yb", [128, 4, D])
        lb.es2 = lb.sb("es2", [128, 4, 4])
        x_out = lb.out("x_out", [TPC, D])
        lb.mod(1.0, (lb.yb[:].rearrange("p a b -> p (a b)"), [("yb", i) for i in range(4)]))
        w_out = lb.inp("w_out", [D, D])
        normw = lb.inp("norm_w", [1, D])
        wout = lb.sb("wout", [128, 8, D], BF16)
        lb.DMA(wout[:], w_out.rearrange("(kc p) n -> p kc n", p=128), w=["wout"], q="pool")
        nwrep = lb.sb("nwrep", [128, D])
        lb.DMA(nwrep[:], normw[0, :].partition_broadcast(128), w=["nwrep"])
        Cb = lb.sb("Cb", [128, 4, 257], BF16)
        qpr = Ring(lb, "qp", [128, 4, 128], BF16, 2)
        qkT = lb.sb("qkT", [128, 8, 128], BF16)
        ST = lb.sb("ST", [128, 4, 128], BF16)
        og = lb.sb("og", [128, D])
        hh = lb.sb("hh", [128, D])
        hd = lb.sb("hd", [128, 16])
        hsb = lb.sb("hsb", [128, D], BF16)
        hsT = lb.sb("hsT", [128, 8, 128], BF16)
        Call = lb.inp("C_all", [NCORES, 128, 1028])
        nGall = lb.inp("nG_all", [NCORES, 4])
        pm = lb.inp("pm", [NCORES, NCORES])
        selr = lb.inp("sel_rep4", [1, 32])
        nGs = lb.sb("nGs", [8, 4])
        pms = lb.sb("pms", [8, 8])
        Rm = lb.sb("Rm", [8, 8, 4])
        wrep = lb.sb("wrep", [128, 32])
        sel = lb.sb("sel", [128, 32])
        callr = Ring(lb, "call", [128, 1028], F32, 2)
        lb.DMA(nGs[:], nGall, w=["nGs"])
        lb.DMA(pms[:], pm, w=["pms"])
        lb.DMA(sel[:], selr[0, :].partition_broadcast(128), w=["sel"])
        lb.TT(Rm[:], pms[:].unsqueeze(2).to_broadcast([8, 8, 4]), nGs[:].unsqueeze(1).to_broadcast([8, 8, 4]),
              ALU.mult, r=["pms", "nGs"], w=["Rm"])
        ps, pk = lb.psum()
        lb.MM(ps[:, 0:32], lb.onesf[0:8, :], Rm[:].rearrange("p a b -> p (a b)"), r=["onesf", "Rm"], w=[pk])
        lb.ACT(wrep[:], ps[:, 0:32], AF.Exp, r=[pk], w=["wrep"], scale=-1.0)
        lb.TT(wrep[:], wrep[:], sel[:], ALU.mult, r=["wrep", "sel"], w=["wrep"])
        for i in range(NCORES):
            cb, cbk = callr.next()
            lb.DMA(cb[:], Call[i], w=[cbk])
            for h in range(4):
                lb.STT(C[:, h, :], cb[:, h * 257:(h + 1) * 257], wrep[:, i * 4 + h:i * 4 + h + 1], C[:, h, :],
                       ALU.mult, ALU.add, r=[cbk, "wrep", ("C", h)], w=[("C", h)])
        for h in range(4):
            lb.CP(Cb[:, h, :], C[:, h, :], r=[("C", h)], w=[("Cb", h)])
    else:
        scr = lb.sb("modscr", [128, 4096])
        lb.mod(1.0, (scr[:], ["modscr"]))
        c_out = lb.out("C_loc", [128, 1028])
        g_out = lb.out("nG", [1, 4])

    for g in range(NG):
        lb.load_xg(g)
        for tt in range(4):
            hm, hmk = hmTr.next()
            lb.prenorm_T(tt, hm, hmk, 0)
            psg, kg = lb.psum()
            for kc in range(8):
                lb.MM(psg[:, 0:8], hm[:, kc, :], win[:, kc, cg:cg + 8], start=(kc == 0), stop=(kc == 7),
                      r=[hmk, wink[kc]], w=[kg])
            gt, gtk = gtr.next()
            lb.TT(gt[:, 0:8], psg[:, 0:8], bg[:], ALU.add, r=[kg, "bg"], w=[gtk])
            lb.ACT(gt[:, 8:12], gt[:, 4:8], AF.Exp, r=[gtk], w=[gtk], scale=-1.0)
            lb.ACT(gt[:, 8:12], gt[:, 8:12], AF.Ln, r=[gtk], w=[gtk], bias=1.0)
            psb_, kb = lb.psum()
            lb.MM(psb_[:, 0:4], lb.trif[:], gt[:, 8:12], r=["trif", gtk], w=[kb])
            lb.MM(psb_[:, 4:8], lb.onesf[:], gt[:, 8:12], r=["onesf", gtk], w=[kb])
            lb.ACT(gt[:, 12:16], psb_[:, 0:4], AF.Exp, r=[kb], w=[gtk], scale=-1.0)
            lb.TT(gt[:, 16:20], gt[:, 0:4], psb_[:, 0:4], ALU.add, r=[gtk, kb], w=[gtk])
            lb.ACT(gt[:, 16:20], gt[:, 16:20], AF.Exp, r=[gtk], w=[gtk])
            lb.TSM(gt[:, 16:20], gt[:, 16:20], float(ML_DK) ** -0.5, r=[gtk], w=[gtk])
            lb.ACT(gt[:, 20:24], psb_[:, 4:8], AF.Exp, r=[kb], w=[gtk], scale=-1.0)
            lb.TT(gacc[:], gacc[:], psb_[:, 4:8], ALU.add, r=["gacc", kb], w=["gacc"])
            psk, kk = lb.psum()
            for kc in range(8):
                lb.MM(psk[:, :], hm[:, kc, :], win[:, kc, ck:ck + 512], start=(kc == 0), stop=(kc == 7),
                      r=[hmk, wink[kc]], w=[kk])
            kp, kpk = kpr.next()
            lb.TT(kp[:], psk[:].rearrange("p (h d) -> p h d", h=4),
                  gt[:, 16:20].unsqueeze(2).to_broadcast([128, 4, 128]), ALU.mult, r=[kk, gtk], w=[kpk])
            ve, vek = vextr.next()
            for half in range(2):
                psv, kv = lb.psum()
                for kc in range(8):
                    lb.MM(psv[:, :], hm[:, kc, :], win[:, kc, cv + half * 512:cv + (half + 1) * 512],
                          start=(kc == 0), stop=(kc == 7), r=[hmk, wink[kc]], w=[kv])
                lb.ACT(ve[:, half * 2:(half + 1) * 2, 0:256], psv[:].rearrange("p (h d) -> p h d", h=2), AF.Copy,
                       r=[kv], w=[vek])
            if full:
                psq, kq = lb.psum()
                for kc in range(8):
                    lb.MM(psq[:, :], hm[:, kc, :], win[:, kc, cq:cq + 512], start=(kc == 0), stop=(kc == 7),
                          r=[hmk, wink[kc]], w=[kq])
                qp, qpk = qpr.next()
                lb.TT(qp[:], psq[:].rearrange("p (h d) -> p h d", h=4),
                      gt[:, 12:16].unsqueeze(2).to_broadcast([128, 4, 128]), ALU.mult, r=[kq, gtk], w=[qpk])
                pst, kt = lb.psum()
                pstb = pst[:].bitcast(BF16)
                for h in range(4):
                    lb.TR(pstb[:, h * 128:(h + 1) * 128], qp[:, h, :], r=[qpk], w=[kt])
                    lb.TR(pstb[:, 512 + h * 128:512 + (h + 1) * 128], kp[:, h, :], r=[kpk], w=[kt])
                lb.ACT(qkT[:], pstb.rearrange("p (a b) -> p a b", a=8), AF.Copy, r=[kt], w=["qkT"])
                psS, kS = lb.psum()
                for h in range(4):
                    lb.MM(psS[:, h * 128:(h + 1) * 128], qkT[:, 4 + h, :], qkT[:, h, :], r=["qkT"], w=[kS])
                lb.TT(ST[:], psS[:].rearrange("p (h t) -> p h t", h=4),
                      lb.trif[:].unsqueeze(1).to_broadcast([128, 4, 128]), ALU.mult, r=[kS, "trif"], w=["ST"])
                for half in range(2):
                    pso, ko = lb.psum()
                    for kc in range(8):
                        lb.MM(pso[:, :], hm[:, kc, :], win[:, kc, co + half * 512:co + (half + 1) * 512],
                              start=(kc == 0), stop=(kc == 7), r=[hmk, wink[kc]], w=[ko])
                    lb.ACT(og[:, half * 512:(half + 1) * 512], pso[:, :], AF.Sigmoid, r=[ko], w=["og"])
                lb.TT(og[:], og[:], nwrep[:], ALU.mult, r=["og", "nwrep"], w=["og"])
                for h in range(4):
                    psO, kO = lb.psum()
                    lb.MM(psO[:, 0:257], ST[:, h, :], ve[:, h, :], start=True, stop=False, r=["ST", vek], w=[kO])
                    lb.MM(psO[:, 0:257], qkT[:, h, :], Cb[:, h, :], start=False, stop=True,
                          r=["qkT", ("Cb", h)], w=[kO])
                    lb.CP(hd[:, h:h + 1], psO[:, 256:257], r=[kO], w=["hd"])
                    lb.STT(hd[:, h:h + 1], hd[:, h:h + 1], -1.0, hd[:, h:h + 1], ALU.mult, ALU.max, r=["hd"], w=["hd"])
                    lb.P.op("dve", lambda e, o=hd[:, h:h + 1]: e.tensor_scalar_max(out=o, in0=o, scalar1=1.0), ["hd"], ["hd"])
                    lb.RECIP(hd[:, 4 + h:5 + h], hd[:, h:h + 1], r=["hd"], w=["hd"])
                    hh_h = hh[:, h * 256:(h + 1) * 256]
                    lb.ACT(hh_h, psO[:, 0:256], AF.Copy, r=[kO, "hd"], w=[("hh", h)], scale=hd[:, 4 + h:5 + h])
                    lb.STT(lb.junk[:, 0:256], hh_h, 1.0, hh_h, ALU.mult, ALU.mult, r=[("hh", h)], w=["junk", "hd"],
                           accum_out=hd[:, 8 + h:9 + h])
                lb.ACT(hd[:, 12:16], hd[:, 8:12], AF.Sqrt, r=["hd"], w=["hd"], scale=1.0 / 256, bias=EPS)
                lb.RECIP(hd[:, 12:16], hd[:, 12:16], r=["hd"], w=["hd"])
                for h in range(4):
                    sl = slice(h * 256, (h + 1) * 256)
                    lb.STT(hsb[:, sl], hh[:, sl], hd[:, 12 + h:13 + h], og[:, sl], ALU.mult, ALU.mult,
                           r=[("hh", h), "hd", "og"], w=["hsb"])
                pst2, kt2 = lb.psum()
                pst2b = pst2[:].bitcast(BF16)
                for kc in range(8):
                    lb.TR(pst2b[:, kc * 128:(kc + 1) * 128], hsb[:, kc * 128:(kc + 1) * 128], r=["hsb"], w=[kt2])
                lb.ACT(hsT[:], pst2b.rearrange("p (a b) -> p a b", a=8), AF.Copy, r=[kt2], w=["hsT"])
                for half in range(2):
                    psy, ky = lb.psum()
                    for kc in range(8):
                        lb.MM(psy[:, :], hsT[:, kc, :], wout[:, kc, half * 512:(half + 1) * 512], start=(kc == 0),
                              stop=(kc == 7), r=["hsT", "wout"], w=[ky])
                    lb.epi_half(tt, half, psy, ky)
                lb.epi_final(tt)
            for h in range(4):
                psU, kU = lb.psum()
                lb.MM(psU[:, 0:257], kp[:, h, :], ve[:, h, :], r=[kpk, vek], w=[kU])
                lb.TSM(C[:, h, :], C[:, h, :], gt[:, 20 + h:21 + h], r=[("C", h), gtk], w=[("C", h)])
                lb.STT(C[:, h, :], psU[:, 0:257], gt[:, 20 + h:21 + h], C[:, h, :], ALU.mult, ALU.add,
                       r=[kU, gtk, ("C", h)], w=[("C", h)])
                if full:
                    lb.CP(Cb[:, h, :], C[:, h, :], r=[("C", h)], w=[("Cb", h)])
        if full:
            lb.store_xg(g, x_out)
    if not full:
        o = lb.DMA(c_out, C[:].rearrange("p a b -> p (a b)"), r=[("C", h) for h in range(4)], w=["c_out_d"])
        lb.out_ops.append(o)
        o = lb.DMA(g_out, gacc[0:1, :], r=["gacc"], w=["g_out_d"])
        lb.out_ops.append(o)
        if lb.fused:
            lb.collect(c_out, lb.io["C_gath"], r=["c_out_d"], w=["c_gath_d"])
            lb.collect(g_out, lb.io["nG_gath"], r=["g_out_d"], w=["g_gath_d"])
    return lb.finish()


MB_GS = 2


def build_mb_tail(lb=None):
    lb = lb or LB()
    lb.consts()
    lb.xgroup(1, 128)
    lb.norm_scratch()
    lb.junkf = lb.sb("junkf", [128, 512])
    scr = lb.sb("modscr", [128, 4096])
    lb.mod(1.0, (scr[:], ["modscr"]))
    w_xbc = lb.inp("w_xbc", [D, 3072])
    tail_out = lb.out("tail", [3, 3072])
    wx = lb.sb("wx", [128, 8, 3072], BF16)
    lb.DMA(wx[:], w_xbc.rearrange("(kc p) n -> p kc n", p=128), w=["wx"], q="pool")
    hmT = lb.sb("hmT", [128, 8, 128], BF16)
    xb = lb.sb("xb", [128, 3072])
    lb.load_xg(0)
    lb.prenorm_T(0, hmT, "hmT", 0)
    for blk in range(6):
        ps, pk = lb.psum()
        for kc in range(8):
            lb.MM(ps[:, :], hmT[:, kc, :], wx[:, kc, blk * 512:(blk + 1) * 512], start=(kc == 0), stop=(kc == 7),
                  r=["hmT", "wx"], w=[pk])
        lb.ACT(xb[:, blk * 512:(blk + 1) * 512], ps[:, :], AF.Copy, r=[pk], w=["xb"])
    o = lb.DMA(tail_out, xb[125:128, :], r=["xb"], w=["tail_d"])
    lb.out_ops.append(o)
    if lb.fused:
        lb.collect(tail_out, lb.io["tail_gath"], r=["tail_d"], w=["tail_gath_d"])
    return lb.finish()


def build_mb(full, lb=None):
    GS = MB_GS
    W = GS * 128
    lb = lb or LB()
    lb.consts()
    slow_in = lb.inp("slow", [128, 128])
    lb.xgroup(GS)
    lb.norm_scratch()
    lb.junkf = lb.sb("junkf", [128, 512])
    ncc = 24 if full else 20
    w_xbc = lb.inp("w_xbc", [D, ncc * 128])
    w_dt = lb.inp("w_dt", [D, 32])
    cw_in = lb.inp("conv_w", [128, 24 * 4])
    cb_in = lb.inp("conv_b", [128, 24])
    tail_in = None if lb.fused else lb.inp("tail", [128, 24 * 3])
    dtb_in = lb.inp("dt_bias", [1, 32])
    alog_in = lb.inp("A_log", [1, 32])
    wdt = lb.sb("wdt", [128, 8, 32], BF16)
    lb.DMA(wdt[:], w_dt.rearrange("(kc p) n -> p kc n", p=128), w=["wdt"], q="pool")
    cw = lb.sb("cw", [128, 24, 4])
    cb = lb.sb("cb", [128, 24])
    carry = lb.sb("carry", [128, 24, 3])
    lb.DMA(cw[:].rearrange("p a b -> p (a b)"), cw_in, w=["cw"])
    lb.DMA(cb[:], cb_in, w=["cb"])
    if not lb.fused:
        lb.DMA(carry[:].rearrange("p a b -> p (a b)"), tail_in, w=[("carry", cc) for cc in range(24)])
    dtb = lb.sb("dtb", [128, 32])
    expA = lb.sb("expA", [128, 32])
    lb.DMA(dtb[:], dtb_in[0, :].partition_broadcast(128), w=["dtb"])
    lb.DMA(expA[:], alog_in[0, :].partition_broadcast(128), w=["expA"])
    lb.ACT(expA[:], expA[:], AF.Exp, r=["expA"], w=["expA"])
    hmT = lb.sb("hmT", [128, 8, W], BF16)
    xprer = Ring(lb, "xpre", [128, W + 3], F32, 2)
    accr = Ring(lb, "acc", [128, W], F32, 2)
    xF = lb.sb("xF", [128, 16, 512], BF16)
    BT = lb.sb("BT", [128, 4, W], BF16)
    dtsr = Ring(lb, "dts", [128, 192], F32, 2)
    xwr = Ring(lb, "xw", [128, 32, 64], BF16, 1)
    Btmr = Ring(lb, "Btm", [128, 4, 128], BF16, 2)
    S = lb.sb("S", [128, 4, 512])
    nacc = lb.sb("nacc", [128, 32])
    lb.MS(nacc[:], 0.0, w=["nacc"])
    lb.MS(S[:], 0.0, w=[("S", q) for q in range(4)])
    xFk = [("xF", cc) for cc in range(16)]
    xf_scr = (xF[:].rearrange("p a b -> p (a b)").bitcast(F32), xFk)
    if lb.fused:
        tsel_in = lb.inp("tail_sel", [24, 3])
        tsel = lb.sb("tsel", [24, 3])
        lb.DMA(tsel[:], tsel_in, w=["tsel"])
        tl = xf_scr[0][0:24, 0:3072]
        lb.DMA(tl, lb.io["tails_all"], w=xFk)
        for cc in range(24):
            ps, pk = lb.psum()
            lb.MM(ps[:, 0:3], tl[:, cc * 128:(cc + 1) * 128], tsel[:], r=xFk + ["tsel"], w=[pk])
            lb.CP(carry[:, cc, :], ps[:, 0:3], r=[pk], w=[("carry", cc)])
    if full:
        slow = lb.sb("slow_sb", [128, 128])
        lb.DMA(slow[:], slow_in, w=["slow"])
        wxr = Ring(lb, "wxb", [128, 8, 256], BF16, 2)
        wzr = Ring(lb, "wzb", [128, 8, 256], BF16, 2)
        lb.rawr = Ring(lb, "raw", [128, 512], F32, 2)
        lb.yb = lb.sb("yb", [128, GS, D])
        lb.es2 = lb.sb("es2", [128, 4, 4])
        x_out = lb.out("x_out", [TPC, D])
        lb.mod(1.0, xf_scr)
        w_z = lb.inp("w_z", [D, 2048])
        w_out = lb.inp("w_out", [2048, D])
        normw = lb.inp("norm_w", [1, 2048])
        d_in = lb.inp("D_skip", [1, 32])
        wout = lb.sb("wout", [128, 16, D], BF16)
        lb.DMA(wout[:], w_out.rearrange("(kc p) n -> p kc n", p=128), w=["wout"], q="pool")
        nwrep = lb.sb("nwrep", [128, 2048])
        lb.DMA(nwrep[:], normw[0, :].partition_broadcast(128), w=["nwrep"])
        Dr = lb.sb("Dr", [128, 32])
        lb.DMA(Dr[:], d_in[0, :].partition_broadcast(128), w=["Dr"])
        CT = lb.sb("CT", [128, 4, W], BF16)
        zs = lb.sb("zs", [128, GS, 2048], BF16)
        Sb = lb.sb("Sb", [128, 4, 512], BF16)
        Rr = lb.sb("Rr", [128, 8, 128])
        E = lb.sb("E", [128, 8, 128], BF16)
        WT = lb.sb("WT", [128, 8, 128], BF16)
        GTt = lb.sb("GT", [128, 4, 128], BF16)
        xs_ = lb.sb("xs_", [128, 32, 64], BF16)
        xD = lb.sb("xD", [128, 32, 64], BF16)
        yt = lb.sb("yt", [128, 4, 512])
        nd = lb.sb("nd", [128, 8])
        ynb = lb.sb("ynb", [128, 2048], BF16)
        ynT = lb.sb("ynT", [128, 16, 128], BF16)
        Sall = lb.inp("S_all", [NCORES, 128, 2048])
        nAall = lb.inp("nA_all", [NCORES, 32])
        pm = lb.inp("pm", [NCORES, NCORES])
        selr = lb.inp("sel_rep32", [1, 256])
        nAs = lb.sb("nAs", [8, 32])
        pms = lb.sb("pms", [8, 8])
        Rm = lb.sb("Rm", [8, 8, 32])
        wrep = lb.sb("wrep", [128, 256])
        sel = lb.sb("sel", [128, 256])
        lb.DMA(nAs[:], nAall, w=["nAs"])
        lb.DMA(pms[:], pm, w=["pms"])
        lb.DMA(sel[:], selr[0, :].partition_broadcast(128), w=["sel"])
        lb.TT(Rm[:], pms[:].unsqueeze(2).to_broadcast([8, 8, 32]), nAs[:].unsqueeze(1).to_broadcast([8, 8, 32]),
              ALU.mult, r=["pms", "nAs"], w=["Rm"])
        ps, pk = lb.psum()
        lb.MM(ps[:, 0:256], lb.onesf[0:8, :], Rm[:].rearrange("p a b -> p (a b)"), r=["onesf", "Rm"], w=[pk])
        lb.ACT(wrep[:], ps[:, 0:256], AF.Exp, r=[pk], w=["wrep"], scale=-1.0)
        lb.TT(wrep[:], wrep[:], sel[:], ALU.mult, r=["wrep", "sel"], w=["wrep"])
        sl_scr = xf_scr[0]
        for i in range(NCORES):
            half = i % 2
            sb_ = sl_scr[:, half * 2048:(half + 1) * 2048]
            sbk = [("xF", cc) for cc in range(half * 8, (half + 1) * 8)]
            lb.DMA(sb_, Sall[i], w=sbk)
            for q in range(4):
                v = sb_[:, q * 512:(q + 1) * 512].rearrange("p (h d) -> p h d", h=8)
                lb.TT(v, v, wrep[:, i * 32 + q * 8:i * 32 + (q + 1) * 8].unsqueeze(2).to_broadcast([128, 8, 64]),
                      ALU.mult, r=sbk + ["wrep"], w=sbk)
                lb.TT(S[:, q, :], S[:, q, :], sb_[:, q * 512:(q + 1) * 512], ALU.add, r=sbk + [("S", q)],
                      w=[("S", q)])
        for q in range(4):
            lb.CP(Sb[:, q, :], S[:, q, :], r=[("S", q)], w=[("Sb", q)])
    else:
        wx = lb.sb("wx", [128, 8, ncc * 128], BF16)
        lb.DMA(wx[:], w_xbc.rearrange("(kc p) n -> p kc n", p=128), w=["wx"], q="pool")
        lb.mod(1.0, xf_scr)
        s_out = lb.out("S_loc", [128, 2048])
        a_out = lb.out("nA", [1, 32])

    for g in range(NT // GS):
        lb.load_xg(g)
        for tt in range(GS):
            lb.prenorm_T(tt, hmT, "hmT", tt)
        if full:
            for blk in range(8):
                wzb, wzk = wzr.next()
                lb.DMA(wzb[:], w_z[:, blk * 256:(blk + 1) * 256].rearrange("(kc p) n -> p kc n", p=128), w=[wzk],
                       q="pool")
                for tt in range(GS):
                    psz, kz = lb.psum()
                    for kc in range(8):
                        lb.MM(psz[:, 0:256], hmT[:, kc, tt * 128:(tt + 1) * 128], wzb[:, kc, :], start=(kc == 0),
                              stop=(kc == 7), r=["hmT", wzk], w=[kz])
                    lb.ACT(zs[:, tt, blk * 256:(blk + 1) * 256], psz[:, 0:256], AF.Silu, r=[kz], w=[("zs", tt)])
        for cc in range(ncc):
            if full:
                if cc % 2 == 0:
                    wxb, wxk = wxr.next()
                    lb.DMA(wxb[:], w_xbc[:, cc * 128:(cc + 2) * 128].rearrange("(kc p) n -> p kc n", p=128),
                           w=[wxk], q="pool")
                wsl = lambda kc, cc=cc, wxb=wxb: wxb[:, kc, (cc % 2) * 128:(cc % 2 + 1) * 128]
            else:
                wxk = "wx"
                wsl = lambda kc, cc=cc: wx[:, kc, cc * 128:(cc + 1) * 128]
            psc, kc_ = lb.psum()
            for kc in range(8):
                lb.MM(psc[:, 0:W], wsl(kc), hmT[:, kc, :], start=(kc == 0), stop=(kc == 7), r=[wxk, "hmT"], w=[kc_])
            xp, xpk = xprer.next()
            lb.CP(xp[:, 0:3], carry[:, cc, :], r=[("carry", cc)], w=[xpk])
            lb.ACT(xp[:, 3:W + 3], psc[:, 0:W], AF.Copy, r=[kc_], w=[xpk])
            lb.CP(carry[:, cc, :], xp[:, W:W + 3], r=[xpk], w=[("carry", cc)])
            ac, ack = accr.next()
            lb.TSM(ac[:], xp[:, 0:W], cw[:, cc, 0:1], r=[xpk, "cw"], w=[ack])
            for j in range(1, 4):
                lb.STT(ac[:], xp[:, j:j + W], cw[:, cc, j:j + 1], ac[:], ALU.mult, ALU.add, r=[xpk, "cw", ack],
                       w=[ack])
            if cc < 16:
                dst, dk = xF[:, cc, 0:W], ("xF", cc)
            elif cc < 20:
                dst, dk = BT[:, cc - 16, :], ("BT", cc - 16)
            else:
                dst, dk = CT[:, cc - 20, :], ("CT", cc - 20)
            lb.ACT(dst, ac[:], AF.Silu, r=[ack, "cb"], w=[dk], bias=cb[:, cc:cc + 1])
        for tt in range(GS):
            tc_ = slice(tt * 128, (tt + 1) * 128)
            psd, kd = lb.psum()
            for kc in range(8):
                lb.MM(psd[:, 0:32], hmT[:, kc, tc_], wdt[:, kc, :], start=(kc == 0), stop=(kc == 7),
                      r=["hmT", "wdt"], w=[kd])
            dt_, dtk = dtsr.next()
            lb.TT(dt_[:, 0:32], psd[:, 0:32], dtb[:], ALU.add, r=[kd, "dtb"], w=[dtk])
            lb.ACT(dt_[:, 0:32], dt_[:, 0:32], AF.Exp, r=[dtk], w=[dtk])
            lb.ACT(dt_[:, 0:32], dt_[:, 0:32], AF.Ln, r=[dtk], w=[dtk], bias=1.0)
            lb.TT(dt_[:, 32:64], dt_[:, 0:32], expA[:], ALU.mult, r=[dtk, "expA"], w=[dtk])
            psa, ka = lb.psum()
            lb.MM(psa[:, 0:32], lb.trif[:], dt_[:, 32:64], r=["trif", dtk], w=[ka])
            lb.MM(psa[:, 32:64], lb.onesf[:], dt_[:, 32:64], r=["onesf", dtk], w=[ka])
            lb.CP(dt_[:, 64:128], psa[:, 0:64], r=[ka], w=[dtk])
            lb.TT(dt_[:, 128:160], dt_[:, 96:128], dt_[:, 64:96], ALU.subtract, r=[dtk], w=[dtk])
            lb.ACT(dt_[:, 128:160], dt_[:, 128:160], AF.Exp, r=[dtk], w=[dtk], scale=-1.0)
            lb.TT(dt_[:, 128:160], dt_[:, 128:160], dt_[:, 0:32], ALU.mult, r=[dtk], w=[dtk])
            lb.TT(nacc[:], nacc[:], dt_[:, 96:128], ALU.add, r=["nacc", dtk], w=["nacc"])
            lb.ACT(dt_[:, 96:128], dt_[:, 96:128], AF.Exp, r=[dtk], w=[dtk], scale=-1.0)
            lb.ACT(dt_[:, 160:192], dt_[:, 64:96], AF.Exp, r=[dtk], w=[dtk], scale=-1.0)
            xw, xwk = xwr.next()
            for half in range(2):
                pst, kt = lb.psum()
                pstb = pst[:].bitcast(BF16)
                for q in range(8):
                    cc = half * 8 + q
                    lb.TR(pstb[:, q * 128:(q + 1) * 128], xF[:, cc, tc_], r=[("xF", cc)], w=[kt])
                hs_ = slice(half * 16, (half + 1) * 16)
                pv = pstb.rearrange("p (h d) -> p h d", h=16)
                lb.TT(xw[:, hs_, :], pv, dt_[:, 128 + half * 16:128 + (half + 1) * 16].unsqueeze(2).to_broadcast(
                    [128, 16, 64]), ALU.mult, r=[kt, dtk], w=[xwk])
                if full:
                    lb.TT(xs_[:, hs_, :], pv, dt_[:, half * 16:(half + 1) * 16].unsqueeze(2).to_broadcast(
                        [128, 16, 64]), ALU.mult, r=[kt, dtk], w=["xs_"])
                    lb.TT(xD[:, hs_, :], pv, Dr[:, hs_].unsqueeze(2).to_broadcast([128, 16, 64]), ALU.mult,
                          r=[kt, "Dr"], w=["xD"])
            pstB, ktB = lb.psum()
            pstBb = pstB[:].bitcast(BF16)
            for q in range(4):
                lb.TR(pstBb[:, q * 128:(q + 1) * 128], BT[:, q, tc_], r=[("BT", q)], w=[ktB])
            Btm, Btmk = Btmr.next()
            lb.ACT(Btm[:], pstBb[:, 0:512].rearrange("p (a b) -> p a b", a=4), AF.Copy, r=[ktB], w=[Btmk])
            if full:
                psG, kG = lb.psum()
                for q in range(4):
                    lb.MM(psG[:, q * 128:(q + 1) * 128], BT[:, q, tc_], CT[:, q, tc_], r=[("BT", q), ("CT", q)],
                          w=[kG])
                lb.TT(GTt[:], psG[:].rearrange("p (a b) -> p a b", a=4),
                      lb.trif[:].unsqueeze(1).to_broadcast([128, 4, 128]), ALU.mult, r=[kG, "trif"], w=["GT"])
                for q in range(4):
                    h8 = slice(q * 8, (q + 1) * 8)
                    lb.TT(Rr[:], dt_[:, 32 + q * 8:32 + (q + 1) * 8].unsqueeze(2).to_broadcast([128, 8, 128]),
                          lb.trif[:].unsqueeze(1).to_broadcast([128, 8, 128]), ALU.mult, r=[dtk, "trif"], w=["Rr"],
                          eng="pool")
                    for half in range(2):
                        psE, kE = lb.psum()
                        lb.MM(psE[:, :], slow[:], Rr[:, half * 4:(half + 1) * 4, :].rearrange("p a b -> p (a b)"),
                              r=["slow", "Rr"], w=[kE])
                        lb.ACT(E[:, half * 4:(half + 1) * 4, :], psE[:].rearrange("p (a b) -> p a b", a=4), AF.Exp,
                               r=[kE], w=["E"], scale=-1.0)
                    lb.TT(WT[:], E[:], GTt[:, q, :].unsqueeze(1).to_broadcast([128, 8, 128]), ALU.mult,
                          r=["E", "GT"], w=["WT"])
                    psY, kY = lb.psum()
                    for hh in range(8):
                        h = q * 8 + hh
                        reg = psY[:, hh * 64:(hh + 1) * 64]
                        lb.MM(reg, lb.identb[:], xD[:, h, :], start=True, stop=False, r=["identb", "xD"], w=[kY])
                        lb.MM(reg, WT[:, hh, :], xs_[:, h, :], start=False, stop=True, r=["WT", "xs_"], w=[kY])
                    psI, kI = lb.psum()
                    lb.MM(psI[:, :], CT[:, q, tc_], Sb[:, q, :], r=[("CT", q), ("Sb", q)], w=[kI])
                    yv = yt[:, q, :].rearrange("p (h d) -> p h d", h=8)
                    lb.TT(yv, psI[:].rearrange("p (h d) -> p h d", h=8),
                          dt_[:, 160 + q * 8:160 + (q + 1) * 8].unsqueeze(2).to_broadcast([128, 8, 64]), ALU.mult,
                          r=[kI, dtk], w=[("yt", q)])
                    lb.TT(yt[:, q, :], yt[:, q, :], psY[:, :], ALU.add, r=[("yt", q), kY], w=[("yt", q)])
                    lb.TT(yt[:, q, :], yt[:, q, :], zs[:, tt, q * 512:(q + 1) * 512], ALU.mult,
                          r=[("yt", q), ("zs", tt)], w=[("yt", q)])
                    lb.STT(lb.junk[:, 0:512], yt[:, q, :], 1.0, yt[:, q, :], ALU.mult, ALU.mult, r=[("yt", q)],
                           w=["junk", "nd"], accum_out=nd[:, q:q + 1])
                lb.ACT(nd[:, 4:8], nd[:, 0:4], AF.Sqrt, r=["nd"], w=["nd"], scale=1.0 / 512, bias=EPS)
                lb.RECIP(nd[:, 4:8], nd[:, 4:8], r=["nd"], w=["nd"])
                for q in range(4):
                    sl = slice(q * 512, (q + 1) * 512)
                    lb.STT(ynb[:, sl], yt[:, q, :], nd[:, 4 + q:5 + q], nwrep[:, sl], ALU.mult, ALU.mult,
                           r=[("yt", q), "nd", "nwrep"], w=["ynb"])
                for half in range(2):
                    pst2, kt2 = lb.psum()
                    pst2b = pst2[:].bitcast(BF16)
                    for q8 in range(8):
                        kc = half * 8 + q8
                        lb.TR(pst2b[:, q8 * 128:(q8 + 1) * 128], ynb[:, kc * 128:(kc + 1) * 128], r=["ynb"], w=[kt2])
                    lb.ACT(ynT[:, half * 8:(half + 1) * 8, :], pst2b.rearrange("p (a b) -> p a b", a=8), AF.Copy,
                           r=[kt2], w=["ynT"])
                for half in range(2):
                    psy, ky = lb.psum()
                    for kc in range(16):
                        lb.MM(psy[:, :], ynT[:, kc, :], wout[:, kc, half * 512:(half + 1) * 512], start=(kc == 0),
                              stop=(kc == 15), r=["ynT", "wout"], w=[ky])
                    lb.epi_half(tt, half, psy, ky)
                lb.epi_final(tt)
            for q in range(4):
                psU, kU = lb.psum()
                lb.MM(psU[:, :], Btm[:, q, :], xw[:, q * 8:(q + 1) * 8, :].rearrange("p h d -> p (h d)"),
                      r=[Btmk, xwk], w=[kU])
                Sg = S[:, q, :].rearrange("p (h d) -> p h d", h=8)
                lb.TT(Sg, Sg, dt_[:, 96 + q * 8:96 + (q + 1) * 8].unsqueeze(2).to_broadcast([128, 8, 64]), ALU.mult,
                      r=[("S", q), dtk], w=[("S", q)])
                lb.TT(S[:, q, :], S[:, q, :], psU[:, :], ALU.add, r=[("S", q), kU], w=[("S", q)])
                if full:
                    lb.CP(Sb[:, q, :], S[:, q, :], r=[("S", q)], w=[("Sb", q)])
        if full:
            lb.store_xg(g, x_out)
    if not full:
        o = lb.DMA(s_out, S[:].rearrange("p a b -> p (a b)"), r=[("S", q) for q in range(4)], w=["s_out_d"])
        lb.out_ops.append(o)
        o = lb.DMA(a_out, nacc[0:1, :], r=["nacc"], w=["a_out_d"])
        lb.out_ops.append(o)
        if lb.fused:
            lb.collect(s_out, lb.io["S_gath"], r=["s_out_d"], w=["s_gath_d"])
            lb.collect(a_out, lb.io["nA_gath"], r=["a_out_d"], w=["a_gath_d"])
    return lb.finish()


def build_fused(layers=(0, 1, 2, 3)):
    Fz = Fused()
    nc = Fz.nc
    x_in = Fz.dram_in("x", [TPC, D])
    x_out = nc.dram_tensor("x_out", [TPC, D], F32, kind="ExternalOutput").ap()
    xa = nc.dram_tensor("xa", [TPC, D], F32).ap()

    def internal(name, shape):
        return nc.dram_tensor(name, list(shape), F32).ap()

    for n, i in enumerate(layers):
        src = x_in if n == 0 else xa
        build_ffn(LB(Fz, "L%df0_" % i, {"x": src, "x_out": xa}))
        if i % 2 == 0:
            cl = internal("L%d_cl" % i, [128, 1028])
            ng = internal("L%d_ng" % i, [1, 4])
            cg = internal("L%d_cg" % i, [NCORES * 128, 1028])
            ngg = internal("L%d_ngg" % i, [NCORES, 4])
            build_ml(False, LB(Fz, "L%dp1_" % i, {"x": xa, "C_loc": cl, "nG": ng, "C_gath": cg, "nG_gath": ngg}))
            build_ml(True, LB(Fz, "L%dp2_" % i, {"x": xa, "x_out": xa, "nG_all": ngg,
                                                  "C_all": cg.rearrange("(r p) c -> r p c", p=128)}))
        else:
            tb = internal("L%d_tb" % i, [3, 3072])
            tg = internal("L%d_tg" % i, [NCORES * 3, 3072])
            sl = internal("L%d_sl" % i, [128, 2048])
            na = internal("L%d_na" % i, [1, 32])
            sg = internal("L%d_sg" % i, [NCORES * 128, 2048])
            nag = internal("L%d_nag" % i, [NCORES, 32])
            build_mb_tail(LB(Fz, "L%dpt_" % i, {"x": xa[TPC - 128:TPC, :], "tail": tb, "tail_gath": tg}))
            build_mb(False, LB(Fz, "L%dp1_" % i, {"x": xa, "tails_all": tg, "S_loc": sl, "nA": na, "S_gath": sg,
                                                   "nA_gath": nag}))
            build_mb(True, LB(Fz, "L%dp2_" % i, {"x": xa, "x_out": xa, "tails_all": tg, "nA_all": nag,
                                                  "S_all": sg.rearrange("(r p) c -> r p c", p=128)}))
        dst = x_out if n == len(layers) - 1 else xa
        build_ffn(LB(Fz, "L%df1_" % i, {"x": xa, "x_out": dst}))
    names = list(Fz.in_names)
    return Fz.close(), names


def fused_inputs(inp, layers=(0, 1, 2, 3)):
    sh = dict(_consts())
    sh["slow"] = np.tril(np.ones((128, 128), np.float32), -1)
    sh["c_l"] = _f32(inp["c"][0].reshape(8, 128).T)

    def put(tag, d):
        for k, v in d.items():
            if k not in SHARED_INPUTS:
                sh[tag + k] = v

    for i in layers:
        j = i // 2
        for f in range(2):
            w1 = inp["ffn_w1"][i, f].reshape(8, 128, NFC, 128)
            w3 = inp["ffn_w3"][i, f].reshape(8, 128, NFC, 128)
            w13 = np.stack([w1, w3], 0)
            d = _mod_inputs(inp, i, 0 if f == 0 else 2)
            d["w13"] = _f32(w13.transpose(3, 2, 0, 1, 4).reshape(NFC, 128, 2048))
            d["w2"] = _f32(inp["ffn_w2"][i, f])
            put("L%df%d_" % (i, f), d)
        m = _mod_inputs(inp, i, 1)
        if i % 2 == 0:
            w_in = inp["ml_w_in"][j]
            d = dict(m)
            d["b_gate"] = _f32(inp["ml_b_gate"][j].reshape(1, 8))
            d1 = dict(d)
            d1["w_in"] = _f32(np.concatenate([w_in[:, 512:2048], w_in[:, 3072:3080]], axis=1))
            put("L%dp1_" % i, d1)
            d2 = dict(d)
            d2["w_in"] = _f32(w_in)
            d2["w_out"] = _f32(inp["ml_w_out"][j])
            d2["norm_w"] = _f32(inp["ml_norm_w"][j].reshape(1, D))
            put("L%dp2_" % i, d2)
        else:
            w_in = inp["mb_w_in"][j]
            dt_ = dict(m)
            dt_["w_xbc"] = _f32(w_in[:, 2048:5120])
            put("L%dpt_" % i, dt_)
            d = dict(m)
            d["w_dt"] = _f32(w_in[:, 5120:5152])
            d["conv_w"] = _f32(inp["mb_conv_w"][j].T.reshape(24, 128, 4).transpose(1, 0, 2).reshape(128, 96))
            d["conv_b"] = _f32(inp["mb_conv_b"][j].reshape(24, 128).T)
            d["dt_bias"] = _f32(inp["mb_dt_bias"][j].reshape(1, 32))
            d["A_log"] = _f32(inp["mb_A_log"][j].reshape(1, 32))
            d1 = dict(d)
            d1["w_xbc"] = _f32(w_in[:, 2048:4608])
            put("L%dp1_" % i, d1)
            d2 = dict(d)
            d2["w_xbc"] = _f32(w_in[:, 2048:5120])
            d2["w_z"] = _f32(w_in[:, 0:2048])
            d2["w_out"] = _f32(inp["mb_w_out"][j])
            d2["norm_w"] = _f32(inp["mb_norm_w"][j].reshape(1, 2048))
            d2["D_skip"] = _f32(inp["mb_D"][j].reshape(1, 32))
            put("L%dp2_" % i, d2)
    return sh


def core_inputs(c):
    pm, sel = _core_masks(c)
    tsel = np.zeros((NCORES * 3, 3), np.float32)
    if c > 0:
        for r in range(3):
            tsel[(c - 1) * 3 + r, r] = 1.0
    return {"pm": pm, "sel_rep4": _f32(np.repeat(sel, 4).reshape(1, 32)),
            "sel_rep32": _f32(np.repeat(sel, 32).reshape(1, 256)), "tail_sel": tsel}


_FUSED = {}


def run_fused(inp, xs, layers=(0, 1, 2, 3)):
    key = tuple(layers)
    if key not in _FUSED:
        _FUSED[key] = build_fused(layers)
    nc, names = _FUSED[key]
    sh = fused_inputs(inp, layers)
    maps = []
    for c in range(NCORES):
        m = dict(sh)
        m.update(core_inputs(c))
        m["x"] = xs[c]
        maps.append({k: m[k] for k in names})
    res = _run(nc, maps)
    return [r["x_out"] for r in res]


_PROGS = {}


def _prog(name, builder):
    if name not in _PROGS:
        _PROGS[name] = builder()
    return _PROGS[name]


def _run(nc, in_maps):
    res = run_bass_kernel_spmd(nc, in_maps, core_ids=list(range(NCORES)))
    return res.results


def _f32(a):
    return np.ascontiguousarray(np.asarray(a, dtype=np.float32))


def _consts():
    ident = np.eye(128, dtype=np.float32)
    tri = np.triu(np.ones((128, 128), dtype=np.float32))
    return {"ident": ident, "tri": tri}


def _mod_inputs(inp, i, s):
    return {
        "c_l": _f32(inp["c"][0].reshape(8, 128).T),
        "ada_w": _f32(inp["ada_w"][i][:, s * 3 * D:(s + 1) * 3 * D]),
        "ada_b": _f32(inp["ada_b"][i][s * 3 * D:(s + 1) * 3 * D].reshape(1, 3 * D)),
        "g_pre": _f32(inp["norm_pre"][i, s].reshape(1, D)),
        "g_post": _f32(inp["norm_post"][i, s].reshape(1, D)),
    }


def run_ffn(inp, xs, i, f):
    nc = _prog("ffn", build_ffn)
    w1 = inp["ffn_w1"][i, f].reshape(8, 128, NFC, 128)
    w3 = inp["ffn_w3"][i, f].reshape(8, 128, NFC, 128)
    w13 = np.stack([w1, w3], 0)
    w13 = _f32(w13.transpose(3, 2, 0, 1, 4).reshape(NFC, 128, 2048))
    base = dict(_consts())
    base.update(_mod_inputs(inp, i, 0 if f == 0 else 2))
    base["w13"] = w13
    base["w2"] = _f32(inp["ffn_w2"][i, f])
    maps = []
    for c in range(NCORES):
        m = dict(base)
        m["x"] = xs[c]
        maps.append(m)
    res = _run(nc, maps)
    return [r["x_out"] for r in res]


def _core_masks(j):
    pm = np.zeros((NCORES, NCORES), np.float32)
    for l in range(NCORES):
        for i in range(NCORES):
            if i < l < j:
                pm[l, i] = 1.0
    sel = np.array([1.0 if i < j else 0.0 for i in range(NCORES)], np.float32)
    return pm, sel


def run_ml(inp, xs, i):
    j = i // 2
    w_in = inp["ml_w_in"][j]
    base = dict(_consts())
    base.update(_mod_inputs(inp, i, 1))
    base["b_gate"] = _f32(inp["ml_b_gate"][j].reshape(1, 8))
    nc1 = _prog("ml1", lambda: build_ml(False))
    b1 = dict(base)
    b1["w_in"] = _f32(np.concatenate([w_in[:, 512:2048], w_in[:, 3072:3080]], axis=1))
    maps = []
    for c in range(NCORES):
        m = dict(b1)
        m["x"] = xs[c]
        maps.append(m)
    r1 = _run(nc1, maps)
    C_all = _f32(np.stack([r["C_loc"] for r in r1], 0))
    nG_all = _f32(np.concatenate([r["nG"] for r in r1], 0))
    nc2 = _prog("ml2", lambda: build_ml(True))
    b2 = dict(base)
    b2["w_in"] = _f32(w_in)
    b2["w_out"] = _f32(inp["ml_w_out"][j])
    b2["norm_w"] = _f32(inp["ml_norm_w"][j].reshape(1, D))
    b2["C_all"] = C_all
    b2["nG_all"] = nG_all
    maps = []
    for c in range(NCORES):
        m = dict(b2)
        m["x"] = xs[c]
        pm, sel = _core_masks(c)
        m["pm"] = pm
        m["sel_rep4"] = _f32(np.repeat(sel, 4).reshape(1, 32))
        maps.append(m)
    r2 = _run(nc2, maps)
    return [r["x_out"] for r in r2]


def run_mb(inp, xs, i):
    j = i // 2
    w_in = inp["mb_w_in"][j]
    base = dict(_consts())
    base["slow"] = np.tril(np.ones((128, 128), np.float32), -1)
    base.update(_mod_inputs(inp, i, 1))
    nct = _prog("mbt", build_mb_tail)
    bt = dict(_consts())
    bt.update(_mod_inputs(inp, i, 1))
    bt["w_xbc"] = _f32(w_in[:, 2048:5120])
    maps = []
    for c in range(NCORES):
        m = dict(bt)
        m["x"] = np.ascontiguousarray(xs[c][TPC - 128:TPC])
        maps.append(m)
    rt = _run(nct, maps)
    tails = [np.zeros((3, 3072), np.float32)] + [rt[c]["tail"] for c in range(NCORES - 1)]
    tails = [_f32(t.T.reshape(24, 128, 3).transpose(1, 0, 2).reshape(128, 72)) for t in tails]
    base["w_dt"] = _f32(w_in[:, 5120:5152])
    base["conv_w"] = _f32(inp["mb_conv_w"][j].T.reshape(24, 128, 4).transpose(1, 0, 2).reshape(128, 96))
    base["conv_b"] = _f32(inp["mb_conv_b"][j].reshape(24, 128).T)
    base["dt_bias"] = _f32(inp["mb_dt_bias"][j].reshape(1, 32))
    base["A_log"] = _f32(inp["mb_A_log"][j].reshape(1, 32))
    nc1 = _prog("mb1", lambda: build_mb(False))
    b1 = dict(base)
    b1["w_xbc"] = _f32(w_in[:, 2048:4608])
    maps = []
    for c in range(NCORES):
        m = dict(b1)
        m["x"] = xs[c]
        m["tail"] = tails[c]
        maps.append(m)
    r1 = _run(nc1, maps)
    S_all = _f32(np.stack([r["S_loc"] for r in r1], 0))
    nA_all = _f32(np.concatenate([r["nA"] for r in r1], 0))
    nc2 = _prog("mb2", lambda: build_mb(True))
    b2 = dict(base)
    b2["w_xbc"] = _f32(w_in[:, 2048:5120])
    b2["w_z"] = _f32(w_in[:, 0:2048])
    b2["w_out"] = _f32(inp["mb_w_out"][j])
    b2["norm_w"] = _f32(inp["mb_norm_w"][j].reshape(1, 2048))
    b2["D_skip"] = _f32(inp["mb_D"][j].reshape(1, 32))
    b2["S_all"] = S_all
    b2["nA_all"] = nA_all
    maps = []
    for c in range(NCORES):
        m = dict(b2)
        m["x"] = xs[c]
        m["tail"] = tails[c]
        pm, sel = _core_masks(c)
        m["pm"] = pm
        m["sel_rep32"] = _f32(np.repeat(sel, 32).reshape(1, 256))
        maps.append(m)
    r2 = _run(nc2, maps)
    return [r["x_out"] for r in r2]


def run_mixer(inp, xs, i):
    if i % 2 == 0:
        return run_ml(inp, xs, i)
    return run_mb(inp, xs, i)


def kernel(**inputs):
    inp = {k: np.asarray(v) for k, v in inputs.items()}
    x = _f32(inp["x"][0])
    xs = [np.ascontiguousarray(x[c * TPC:(c + 1) * TPC]) for c in range(NCORES)]
    xs = run_fused(inp, xs)
    return np.concatenate(xs, 0).reshape(1, SEQ, D).astype(np.float32)
```

```python
import contextlib
import math
import numpy as np
import concourse.bass as bass
import concourse.mybir as mybir
from concourse.bass_utils import run_bass_kernel_spmd

F32 = mybir.dt.float32
BF16 = mybir.dt.bfloat16
AF = mybir.ActivationFunctionType
ALU = mybir.AluOpType

import os
NCORES = 8
TPC = int(os.environ.get("K_TPC", "2048"))
SEQ = TPC * NCORES
NT = TPC // 128
NG = NT // 4
D = 1024
DFF = 2816
NFC = DFF // 128
EPS = 1e-6
SAME_ENGINE_SYNC = True
DMA_RING = 8


class Prog:
    ENGS = ("pe", "act", "dve", "pool", "sp")

    def __init__(self, nc):
        self.nc = nc
        self.ops = []
        self.last_w = {}
        self.readers = {}

    def op(self, eng, fn, reads=(), writes=(), dma=False):
        idx = len(self.ops)
        deps = set()
        for k in reads:
            w = self.last_w.get(k)
            if w is not None:
                deps.add(w)
        for k in writes:
            w = self.last_w.get(k)
            if w is not None:
                deps.add(w)
            for r in self.readers.get(k, ()):
                deps.add(r)
        deps.discard(idx)
        for k in reads:
            self.readers.setdefault(k, []).append(idx)
        for k in writes:
            self.last_w[k] = idx
            self.readers[k] = []
        self.ops.append(dict(eng=eng, fn=fn, deps=deps, dma=dma))
        return idx

    def emit(self, ctx, final_wait_ops=()):
        nc = self.nc
        ops = self.ops
        n = len(ops)
        need_sig = [False] * n
        for i, o in enumerate(ops):
            for d in o["deps"]:
                p = ops[d]
                if p["dma"]:
                    continue
                if p["eng"] == o["eng"] and not o["dma"]:
                    if p["eng"] == "pe" or not SAME_ENGINE_SYNC:
                        continue
                need_sig[d] = True
        for d in final_wait_ops:
            if not ops[d]["dma"]:
                need_sig[d] = True
        last = {}
        for i, o in enumerate(ops):
            if not o["dma"]:
                last[o["eng"]] = i
        for i in last.values():
            need_sig[i] = True
        cnt, dcnt = ctx.cnt, ctx.dcnt
        for i, o in enumerate(ops):
            e = o["eng"]
            if o["dma"]:
                k = dcnt[e]
                dcnt[e] += 1
                o["sem"] = ("d", e, k % DMA_RING)
                o["val"] = 16 * (k // DMA_RING + 1)
                o["dma_k"] = k
            elif need_sig[i]:
                cnt[e] += 1
                o["sem"] = ("e", e)
                o["val"] = cnt[e]
            else:
                o["sem"] = None
        per_eng = {e: [i for i, o in enumerate(ops) if o["eng"] == e] for e in self.ENGS}
        self.stats = {e: len(per_eng[e]) for e in self.ENGS}
        sems = ctx.get_sems(nc)
        targets = []
        for e in self.ENGS:
            if cnt[e] > 0:
                targets.append((("e", e), cnt[e]))
            for r in range(DMA_RING):
                if dcnt[e] > r:
                    targets.append((("d", e, r), 16 * ((dcnt[e] - r + DMA_RING - 1) // DMA_RING)))
        self.sched = {e: [] for e in self.ENGS}
        with nc.Block() as block:

            def replay(e, eng):
                waited = ctx.waited[e]
                cur = []

                def wait(semkey, val):
                    if waited.get(semkey, 0) >= val:
                        return
                    eng.wait_ge(sems[semkey], val)
                    waited[semkey] = val
                    cur.append((semkey, val))

                for i in per_eng[e]:
                    o = ops[i]
                    for d in sorted(o["deps"]):
                        p = ops[d]
                        if p["eng"] == e and not p["dma"] and not o["dma"]:
                            if e == "pe" or not SAME_ENGINE_SYNC:
                                continue
                        wait(p["sem"], p["val"])
                    if o["dma"] and o["dma_k"] >= DMA_RING:
                        wait(o["sem"], o["val"] - 16)
                    ins = o["fn"](eng)
                    if o["sem"] is not None:
                        ins.then_inc(sems[o["sem"]], 16 if o["dma"] else 1)
                    self.sched[e].append((i, list(cur), o["sem"], 16 if o["dma"] else 1))
                    del cur[:]
                for semkey, val in targets:
                    wait(semkey, val)
                self.sched[e].append((-1, list(cur), None, 0))

            names = {"pe": "tensor", "act": "scalar", "dve": "vector", "pool": "gpsimd", "sp": "sync"}
            for e in self.ENGS:
                getattr(block, names[e])(lambda eng, e=e: replay(e, eng))


class Ctx:
    def __init__(self):
        self.cnt = {e: 0 for e in Prog.ENGS}
        self.dcnt = {e: 0 for e in Prog.ENGS}
        self.waited = {e: {} for e in Prog.ENGS}
        self.simval = {}
        self.sems = None
        self.es = contextlib.ExitStack()

    def get_sems(self, nc):
        if self.sems is None:
            self.sems = {}
            for e in Prog.ENGS:
                self.sems[("e", e)] = self.es.enter_context(nc.semaphore("s_" + e))
            for e in ("sp", "pool"):
                for r in range(DMA_RING):
                    self.sems[("d", e, r)] = self.es.enter_context(nc.semaphore("d_%s_%d" % (e, r)))
        return self.sems


def simulate(P, val):
    pos = {e: 0 for e in P.ENGS}
    prog = True
    while prog:
        prog = False
        for e in P.ENGS:
            while pos[e] < len(P.sched[e]):
                i, waits, sem, inc = P.sched[e][pos[e]]
                if all(val.get(k, 0) >= v for k, v in waits):
                    if sem is not None:
                        val[sem] = val.get(sem, 0) + inc
                    pos[e] += 1
                    prog = True
                else:
                    break
    bad = {e: P.sched[e][pos[e]] for e in P.ENGS if pos[e] < len(P.sched[e])}
    return bad


class Ring:
    def __init__(self, lb, name, shape, dt, n):
        self.t = [lb.sb("%s%d" % (name, i), shape, dt) for i in range(n)]
        self.k = ["%s%d" % (name, i) for i in range(n)]
        self.i = 0

    def next(self):
        i = self.i
        self.i = (i + 1) % len(self.t)
        return self.t[i], self.k[i]


class Fused:
    def __init__(self):
        self.nc = bass.Bass("TRN2", target_bir_lowering=False)
        self.es = contextlib.ExitStack()
        self.ctx = Ctx()
        self.banks = [self.es.enter_context(self.nc.psum_tensor("bank%d" % i, [128, 512], F32)) for i in range(8)]
        self.bi = 0
        self.declared = {}
        self.in_names = []

    def dram_in(self, name, shape, dt=F32):
        if name not in self.declared:
            self.declared[name] = self.nc.dram_tensor(name, list(shape), dt, kind="ExternalInput").ap()
            self.in_names.append(name)
        return self.declared[name]

    def close(self):
        self.ctx.es.close()
        self.es.close()
        return self.nc


SHARED_INPUTS = ("ident", "tri", "slow", "c_l", "pm", "sel_rep4", "sel_rep32", "tail_sel")


class LB:
    def __init__(self, parent=None, tag="", io=None):
        self.fused = parent is not None
        self.parent = parent if parent is not None else Fused()
        self.tag = tag
        self.io = io or {}
        self.nc = self.parent.nc
        self.es = contextlib.ExitStack()
        self.P = Prog(self.nc)
        self.out_ops = []
        self.banks = self.parent.banks

    def inp(self, name, shape, dt=F32):
        if name in self.io:
            return self.io[name]
        full = name if name in SHARED_INPUTS else self.tag + name
        return self.parent.dram_in(full, shape, dt)

    def out(self, name, shape, dt=F32):
        if name in self.io:
            return self.io[name]
        return self.nc.dram_tensor(self.tag + name, list(shape), dt, kind="ExternalOutput").ap()

    def sb(self, name, shape, dt=F32):
        return self.es.enter_context(self.nc.sbuf_tensor(self.tag + name, list(shape), dt))

    def psum(self):
        i = self.parent.bi
        self.parent.bi = (i + 1) % 8
        return self.banks[i], ("ps", i)

    def collect(self, src, dst, r=(), w=()):
        o = self.P.op("pool", lambda e: e.collective_compute(
            "AllGather", ALU.bypass, replica_groups=[list(range(NCORES))], ins=[src.opt()], outs=[dst.opt()]), r, w)
        self.out_ops.append(o)

    def finish(self):
        ctx = self.parent.ctx
        self.P.emit(ctx, self.out_ops)
        bad = simulate(self.P, ctx.simval)
        print("PROG", self.tag, self.P.stats, "deadlock" if bad else "sim-ok", bad if bad else "")
        assert not bad
        self.es.close()
        if not self.fused:
            return self.parent.close()
        return None

    def MM(self, out, lhsT, rhs, start=True, stop=True, r=(), w=()):
        self.P.op("pe", lambda e: e.matmul(out, lhsT, rhs, start=start, stop=stop), r, w)

    def TR(self, out, in_, r=(), w=()):
        idb = self.identb
        self.P.op("pe", lambda e: e.transpose(out, in_, idb[:]), list(r) + ["identb"], w)

    def ACT(self, out, in_, func, r=(), w=(), **kw):
        self.P.op("act", lambda e: e.activation(out=out, in_=in_, func=func, **kw), r, w)

    def TT(self, out, in0, in1, op, r=(), w=(), eng="dve"):
        self.P.op(eng, lambda e: e.tensor_tensor(out=out, in0=in0, in1=in1, op=op), r, w)

    def STT(self, out, in0, scalar, in1, op0, op1, r=(), w=(), accum_out=None, eng="dve"):
        if accum_out is None:
            self.P.op(eng, lambda e: e.scalar_tensor_tensor(out=out, in0=in0, scalar=scalar, in1=in1, op0=op0, op1=op1), r, w)
        else:
            self.P.op(eng, lambda e: e.scalar_tensor_tensor(out=out, in0=in0, scalar=scalar, in1=in1, op0=op0, op1=op1,
                                                            accum_out=accum_out), r, w)

    def TSM(self, out, in0, s1, r=(), w=(), eng="dve"):
        self.P.op(eng, lambda e: e.tensor_scalar_mul(out=out, in0=in0, scalar1=s1), r, w)

    def TSA(self, out, in0, s1, r=(), w=(), eng="dve"):
        self.P.op(eng, lambda e: e.tensor_scalar_add(out=out, in0=in0, scalar1=s1), r, w)

    def RECIP(self, out, in_, r=(), w=()):
        self.P.op("dve", lambda e: e.reciprocal(out=out, in_=in_), r, w)

    def CP(self, out, in_, r=(), w=(), eng="dve"):
        self.P.op(eng, lambda e: e.tensor_copy(out=out, in_=in_), r, w)

    def MS(self, ap, val, w=(), eng="dve"):
        self.P.op(eng, lambda e: e.memset(ap, val), (), w)

    def DMA(self, out, in_, r=(), w=(), q="sp"):
        return self.P.op(q, lambda e: e.dma_start(out=out, in_=in_), r, w, dma=True)

    def consts(self):
        ident = self.inp("ident", [128, 128])
        tri = self.inp("tri", [128, 128])
        self.identf = self.sb("identf", [128, 128])
        self.identb = self.sb("identb", [128, 128], BF16)
        self.trif = self.sb("trif", [128, 128])
        self.onesf = self.sb("onesf", [128, 128])
        self.DMA(self.identf[:], ident, w=["identf"])
        self.DMA(self.identb[:], ident, w=["identb"], q="pool")
        self.DMA(self.trif[:], tri, w=["trif"])
        self.MS(self.onesf[:], 1.0, w=["onesf"])

    def xgroup(self, gs=4, rows=TPC):
        self.gs = gs
        self.x_in = self.inp("x", [rows, D])
        self.xg = self.sb("xg", [128, gs, D])

    def load_xg(self, g):
        n = self.gs * 128
        self.DMA(self.xg[:], self.x_in[g * n:(g + 1) * n, :].rearrange("(t p) d -> p t d", p=128),
                 w=[("x", i) for i in range(self.gs)])

    def store_xg(self, g, x_out):
        n = self.gs * 128
        o = self.DMA(x_out[g * n:(g + 1) * n, :].rearrange("(t p) d -> p t d", p=128), self.xg[:],
                     r=[("x", i) for i in range(self.gs)])
        self.out_ops.append(o)

    def norm_scratch(self):
        self.hmf = self.sb("hmf", [128, D])
        self.hmb = Ring(self, "hmb", [128, D], BF16, 2)
        self.junk = self.sb("junk", [128, D], BF16)
        self.ssr = Ring(self, "ss", [128, 4], F32, 2)

    def mod(self, weight, scratch_f32):
        gpre = self.inp("g_pre", [1, D])
        gpost = self.inp("g_post", [1, D])
        A = self.sb("modA", [128, D])
        SH = self.sb("modSH", [128, D])
        G = self.sb("modG", [128, D])
        st, stk = self.hmf, "hmf"
        self.A, self.SH, self.G = A, SH, G
        if "modall" in self.io:
            flat = self.io["modall"].rearrange("a b -> (a b)")
            base = self.io["mod_base"]
            for j, (dst, key) in enumerate(((SH, "SH"), (A, "A"), (G, "G"))):
                self.DMA(dst[:], flat[base + j * D:base + (j + 1) * D].partition_broadcast(128), w=[key])
            self.DMA(st[:], gpre[0, :].partition_broadcast(128), w=[stk])
            self.STT(A[:], A[:], 1.0, st[:], ALU.add, ALU.mult, r=["A", stk], w=["A"])
            self.DMA(st[:], gpost[0, :].partition_broadcast(128), w=[stk])
            self.STT(G[:], G[:], float(weight), st[:], ALU.mult, ALU.mult, r=["G", stk], w=["G"])
            return
        c_in = self.inp("c_l", [128, 8])
        adaw = self.inp("ada_w", [D, 3 * D])
        adab = self.inp("ada_b", [1, 3 * D])
        cs = self.sb("cs", [128, 8])
        scr, scrk = scratch_f32
        self.DMA(cs[:], c_in, w=["cs"])
        self.ACT(cs[:], cs[:], AF.Silu, r=["cs"], w=["cs"])
        dsts = ((SH, "SH"), (A, "A"), (G, "G"))
        for j, (dst, key) in enumerate(dsts):
            self.DMA(dst[:], adab[0, j * D:(j + 1) * D].partition_broadcast(128), w=[key])
        for j, (dst, key) in enumerate(dsts):
            for b in range(2):
                col = j * D + b * 512
                wb = scr.rearrange("p (k n) -> p k n", k=8)
                self.DMA(wb, adaw[:, col:col + 512].rearrange("(kc p) n -> p kc n", p=128), w=scrk)
                acc = self.junkf
                self.TSM(acc[:], wb[:, 0, :], cs[:, 0:1], r=list(scrk) + ["cs"], w=["junkf"])
                for kc in range(1, 8):
                    self.STT(acc[:], wb[:, kc, :], cs[:, kc:kc + 1], acc[:], ALU.mult, ALU.add,
                             r=list(scrk) + ["cs", "junkf"], w=["junkf"])
                ps, pk = self.psum()
                self.MM(ps[:, :], self.onesf[:], acc[:], r=["onesf", "junkf"], w=[pk])
                sl = dst[:, b * 512:(b + 1) * 512]
                self.TT(sl, ps[:, :], sl, ALU.add, r=[pk, key], w=[key])
        self.DMA(st[:], gpre[0, :].partition_broadcast(128), w=[stk])
        self.STT(A[:], A[:], 1.0, st[:], ALU.add, ALU.mult, r=["A", stk], w=["A"])
        self.DMA(st[:], gpost[0, :].partition_broadcast(128), w=[stk])
        self.STT(G[:], G[:], float(weight), st[:], ALU.mult, ALU.mult, r=["G", stk], w=["G"])

    def prenorm_T(self, tt, hmT, hmTk, col):
        x_t = self.xg[:, tt, :]
        xk = ("x", tt)
        ss, ssk = self.ssr.next()
        hmb, hmbk = self.hmb.next()
        self.STT(self.junk[:], x_t, 1.0, x_t, ALU.mult, ALU.mult, r=[xk], w=["junk", ssk], accum_out=ss[:, 0:1])
        self.ACT(ss[:, 1:2], ss[:, 0:1], AF.Sqrt, r=[ssk], w=[ssk], scale=1.0 / D, bias=EPS)
        self.RECIP(ss[:, 2:3], ss[:, 1:2], r=[ssk], w=[ssk])
        self.STT(self.hmf[:], x_t, ss[:, 2:3], self.A[:], ALU.mult, ALU.mult, r=[xk, ssk, "A"], w=["hmf"])
        self.TT(hmb[:], self.hmf[:], self.SH[:], ALU.add, r=["hmf", "SH"], w=[hmbk])
        ps, pk = self.psum()
        psb = ps[:].bitcast(BF16)
        for kc in range(8):
            self.TR(psb[:, kc * 128:(kc + 1) * 128], hmb[:, kc * 128:(kc + 1) * 128], r=[hmbk], w=[pk])
        self.ACT(hmT[:, :, col * 128:(col + 1) * 128], psb.rearrange("p (a b) -> p a b", a=8), AF.Copy,
                 r=[pk], w=[hmTk])

    def epi_alloc(self):
        self.yb = self.sb("yb", [128, 4, D])
        self.es2 = self.sb("es2", [128, 4, 4])
        self.junkf = self.sb("junkf", [128, 512])
        self.rawr = Ring(self, "raw", [128, 512], F32, 2)

    def epi_half(self, tt, h, ps, pk, slot=None):
        sl = tt if slot is None else slot
        raw, rk = self.rawr.next()
        self.ACT(raw[:], ps[:, :], AF.Copy, r=[pk], w=[rk])
        self.STT(self.junkf[:], raw[:], 1.0, raw[:], ALU.mult, ALU.mult, r=[rk], w=["junkf", ("es2", sl)],
                 accum_out=self.es2[:, sl, h:h + 1])
        self.TT(self.yb[:, sl, h * 512:(h + 1) * 512], ps[:, :], self.G[:, h * 512:(h + 1) * 512], ALU.mult,
                r=[pk, "G"], w=[("yb", sl)])

    def epi_final(self, tt, slot=None):
        sl = tt if slot is None else slot
        e = self.es2
        k = ("es2", sl)
        self.TT(e[:, sl, 2:3], e[:, sl, 0:1], e[:, sl, 1:2], ALU.add, r=[k], w=[k])
        self.ACT(e[:, sl, 3:4], e[:, sl, 2:3], AF.Sqrt, r=[k], w=[k], scale=1.0 / D, bias=EPS)
        self.RECIP(e[:, sl, 2:3], e[:, sl, 3:4], r=[k], w=[k])
        self.STT(self.xg[:, tt, :], self.yb[:, sl, :], e[:, sl, 2:3], self.xg[:, tt, :], ALU.mult, ALU.add,
                 r=[k, ("yb", sl), ("x", tt)], w=[("x", tt)])


FFN_GS = 8


def build_ffn(lb=None):
    lb = lb or LB()
    GS = FFN_GS
    W = GS * 128
    lb.consts()
    lb.xgroup(GS)
    lb.norm_scratch()
    lb.epi_alloc()
    x_out = lb.out("x_out", [TPC, D])
    w13 = lb.inp("w13", [NFC, 128, 2048])
    w2 = lb.inp("w2", [DFF, D])
    gT = lb.sb("gT", [128, NFC, W], BF16)
    hmT = lb.sb("hmT", [128, 8, W], BF16)
    w13r = Ring(lb, "w13b", [128, 2048], BF16, 2)
    w2sb = lb.sb("w2sb", [128, NFC, D], BF16)
    silr = Ring(lb, "sil", [128, 512], F32, 2)
    lb.mod(0.5, (lb.yb[:].rearrange("p a b -> p (a b)"), [("yb", i) for i in range(4)]))
    for c in range(NFC):
        lb.DMA(w2sb[:, c, :], w2[c * 128:(c + 1) * 128, :], w=[("w2", c)], q="pool")
    for g in range(NT // GS):
        lb.load_xg(g)
        for tt in range(GS):
            lb.prenorm_T(tt, hmT, ("hmT", tt // 4), tt)
        for c in range(NFC):
            wb, wk = w13r.next()
            lb.DMA(wb[:], w13[c], w=[wk], q="pool")
            for sub in range(W // 512):
                cs_ = slice(sub * 512, (sub + 1) * 512)
                ps1, k1 = lb.psum()
                ps3, k3 = lb.psum()
                for kc in range(8):
                    lb.MM(ps1[:, :], wb[:, kc * 128:(kc + 1) * 128], hmT[:, kc, cs_], start=(kc == 0),
                          stop=(kc == 7), r=[wk, ("hmT", sub)], w=[k1])
                for kc in range(8):
                    lb.MM(ps3[:, :], wb[:, 1024 + kc * 128:1024 + (kc + 1) * 128], hmT[:, kc, cs_],
                          start=(kc == 0), stop=(kc == 7), r=[wk, ("hmT", sub)], w=[k3])
                sl, sk = silr.next()
                lb.ACT(sl[:], ps1[:, :], AF.Silu, r=[k1], w=[sk])
                lb.TT(gT[:, c, cs_], sl[:], ps3[:, :], ALU.mult, r=[sk, k3], w=[("gT", c, sub)])
        for tt in range(GS):
            slot = tt % 4
            for h in range(2):
                ps, pk = lb.psum()
                for c in range(NFC):
                    lb.MM(ps[:, :], gT[:, c, tt * 128:(tt + 1) * 128], w2sb[:, c, h * 512:(h + 1) * 512],
                          start=(c == 0), stop=(c == NFC - 1), r=[("gT", c, tt // 4), ("w2", c)], w=[pk])
                lb.epi_half(tt, h, ps, pk, slot)
            lb.epi_final(tt, slot)
        lb.store_xg(g, x_out)
    return lb.finish()


ML_DK = 128


def build_ml(full, lb=None):
    lb = lb or LB()
    lb.consts()
    lb.xgroup()
    lb.norm_scratch()
    lb.junkf = lb.sb("junkf", [128, 512])
    ncol = 3080 if full else 1544
    if full:
        cq, ck, cv, co, cg = 0, 512, 1024, 2048, 3072
    else:
        ck, cv, cg = 0, 512, 1536
    w_in = lb.inp("w_in", [D, ncol])
    bgate = lb.inp("b_gate", [1, 8])
    win = lb.sb("win", [128, 8, ncol], BF16)
    for kc in range(8):
        lb.DMA(win[:, kc, :], w_in[kc * 128:(kc + 1) * 128, :], w=[("win", kc)], q="pool")
    wink = [("win", kc) for kc in range(8)]
    bg = lb.sb("bg", [128, 8])
    lb.DMA(bg[:], bgate[0, :].partition_broadcast(128), w=["bg"])
    C = lb.sb("C", [128, 4, 257])
    hmTr = Ring(lb, "hmT", [128, 8, 128], BF16, 2)
    gtr = Ring(lb, "gt", [128, 24], F32, 2)
    kpr = Ring(lb, "kp", [128, 4, 128], BF16, 2)
    vextr = Ring(lb, "vext", [128, 4, 257], BF16, 2)
    for i in range(2):
        lb.MS(vextr.t[i][:, :, 256:257], 1.0, w=[vextr.k[i]])
    gacc = lb.sb("gacc", [128, 4])
    lb.MS(gacc[:], 0.0, w=["gacc"])
    lb.MS(C[:], 0.0, w=[("C", h) for h in range(4)])
    if full:
        lb.rawr = Ring(lb, "raw", [128, 512], F32, 2)
        lb.yb = lb.sb("yb", [128, 4, D])
        lb.es2 = lb.sb("es2", [128, 4, 4])
        x_out = lb.out("x_out", [TPC, D])
        lb.mod(1.0, (lb.yb[:].rearrange("p a b -> p (a b)"), [("yb", i) for i in range(4)]))
        w_out = lb.inp("w_out", [D, D])
        normw = lb.inp("norm_w", [1, D])
        wout = lb.sb("wout", [128, 8, D], BF16)
        lb.DMA(wout[:], w_out.rearrange("(kc p) n -> p kc n", p=128), w=["wout"], q="pool")
        nwrep = lb.sb("nwrep", [128, D])
        lb.DMA(nwrep[:], normw[0, :].partition_broadcast(128), w=["nwrep"])
        Cb = lb.sb("Cb", [128, 4, 257], BF16)
        qpr = Ring(lb, "qp", [128, 4, 128], BF16, 2)
        qkT = lb.sb("qkT", [128, 8, 128], BF16)
        ST = lb.sb("ST", [128, 4, 128], BF16)
        og = lb.sb("og", [128, D])
        hh = lb.sb("hh", [128, D])
        hd = lb.sb("hd", [128, 16])
        hsb = lb.sb("hsb", [128, D], BF16)
        hsT = lb.sb("hsT", [128, 8, 128], BF16)
        Call = lb.inp("C_all", [NCORES, 128, 1028])
        nGall = lb.inp("nG_all", [NCORES, 4])
        pm = lb.inp("pm", [NCORES, NCORES])
        selr = lb.inp("sel_rep4", [1, 32])
        nGs = lb.sb("nGs", [8, 4])
        pms = lb.sb("pms", [8, 8])
        Rm = lb.sb("Rm", [8, 8, 4])
        wrep = lb.sb("wrep", [128, 32])
        sel = lb.sb("sel", [128, 32])
        callr = Ring(lb, "call", [128, 1028], F32, 2)
        lb.DMA(nGs[:], nGall, w=["nGs"])
        lb.DMA(pms[:], pm, w=["pms"])
        lb.DMA(sel[:], selr[0, :].partition_broadcast(128), w=["sel"])
        lb.TT(Rm[:], pms[:].unsqueeze(2).to_broadcast([8, 8, 4]), nGs[:].unsqueeze(1).to_broadcast([8, 8, 4]),
              ALU.mult, r=["pms", "nGs"], w=["Rm"])
        ps, pk = lb.psum()
        lb.MM(ps[:, 0:32], lb.onesf[0:8, :], Rm[:].rearrange("p a b -> p (a b)"), r=["onesf", "Rm"], w=[pk])
        lb.ACT(wrep[:], ps[:, 0:32], AF.Exp, r=[pk], w=["wrep"], scale=-1.0)
        lb.TT(wrep[:], wrep[:], sel[:], ALU.mult, r=["wrep", "sel"], w=["wrep"])
        for i in range(NCORES):
            cb, cbk = callr.next()
            lb.DMA(cb[:], Call[i], w=[cbk])
            for h in range(4):
                lb.STT(C[:, h, :], cb[:, h * 257:(h + 1) * 257], wrep[:, i * 4 + h:i * 4 + h + 1], C[:, h, :],
                       ALU.mult, ALU.add, r=[cbk, "wrep", ("C", h)], w=[("C", h)])
        for h in range(4):
            lb.CP(Cb[:, h, :], C[:, h, :], r=[("C", h)], w=[("Cb", h)])
    else:
        scr = lb.sb("modscr", [128, 4096])
        lb.mod(1.0, (scr[:], ["modscr"]))
        c_out = lb.out("C_loc", [128, 1028])
        g_out = lb.out("nG", [1, 4])

    for g in range(NG):
        lb.load_xg(g)
        for tt in range(4):
            hm, hmk = hmTr.next()
            lb.prenorm_T(tt, hm, hmk, 0)
            psg, kg = lb.psum()
            for kc in range(8):
                lb.MM(psg[:, 0:8], hm[:, kc, :], win[:, kc, cg:cg + 8], start=(kc == 0), stop=(kc == 7),
                      r=[hmk, wink[kc]], w=[kg])
            gt, gtk = gtr.next()
            lb.TT(gt[:, 0:8], psg[:, 0:8], bg[:], ALU.add, r=[kg, "bg"], w=[gtk])
            lb.ACT(gt[:, 8:12], gt[:, 4:8], AF.Exp, r=[gtk], w=[gtk], scale=-1.0)
            lb.ACT(gt[:, 8:12], gt[:, 8:12], AF.Ln, r=[gtk], w=[gtk], bias=1.0)
            psb_, kb = lb.psum()
            lb.MM(psb_[:, 0:4], lb.trif[:], gt[:, 8:12], r=["trif", gtk], w=[kb])
            lb.MM(psb_[:, 4:8], lb.onesf[:], gt[:, 8:12], r=["onesf", gtk], w=[kb])
            lb.ACT(gt[:, 12:16], psb_[:, 0:4], AF.Exp, r=[kb], w=[gtk], scale=-1.0)
            lb.TT(gt[:, 16:20], gt[:, 0:4], psb_[:, 0:4], ALU.add, r=[gtk, kb], w=[gtk])
            lb.ACT(gt[:, 16:20], gt[:, 16:20], AF.Exp, r=[gtk], w=[gtk])
            lb.TSM(gt[:, 16:20], gt[:, 16:20], float(ML_DK) ** -0.5, r=[gtk], w=[gtk])
            lb.ACT(gt[:, 20:24], psb_[:, 4:8], AF.Exp, r=[kb], w=[gtk], scale=-1.0)
            lb.TT(gacc[:], gacc[:], psb_[:, 4:8], ALU.add, r=["gacc", kb], w=["gacc"])
            psk, kk = lb.psum()
            for kc in range(8):
                lb.MM(psk[:, :], hm[:, kc, :], win[:, kc, ck:ck + 512], start=(kc == 0), stop=(kc == 7),
                      r=[hmk, wink[kc]], w=[kk])
            kp, kpk = kpr.next()
            lb.TT(kp[:], psk[:].rearrange("p (h d) -> p h d", h=4),
                  gt[:, 16:20].unsqueeze(2).to_broadcast([128, 4, 128]), ALU.mult, r=[kk, gtk], w=[kpk])
            ve, vek = vextr.next()
            for half in range(2):
                psv, kv = lb.psum()
                for kc in range(8):
                    lb.MM(psv[:, :], hm[:, kc, :], win[:, kc, cv + half * 512:cv + (half + 1) * 512],
                          start=(kc == 0), stop=(kc == 7), r=[hmk, wink[kc]], w=[kv])
                lb.ACT(ve[:, half * 2:(half + 1) * 2, 0:256], psv[:].rearrange("p (h d) -> p h d", h=2), AF.Copy,
                       r=[kv], w=[vek])
            if full:
                psq, kq = lb.psum()
                for kc in range(8):
                    lb.MM(psq[:, :], hm[:, kc, :], win[:, kc, cq:cq + 512], start=(kc == 0), stop=(kc == 7),
                          r=[hmk, wink[kc]], w=[kq])
                qp, qpk = qpr.next()
                lb.TT(qp[:], psq[:].rearrange("p (h d) -> p h d", h=4),
                      gt[:, 12:16].unsqueeze(2).to_broadcast([128, 4, 128]), ALU.mult, r=[kq, gtk], w=[qpk])
                pst, kt = lb.psum()
                pstb = pst[:].bitcast(BF16)
                for h in range(4):
                    lb.TR(pstb[:, h * 128:(h + 1) * 128], qp[:, h, :], r=[qpk], w=[kt])
                    lb.TR(pstb[:, 512 + h * 128:512 + (h + 1) * 128], kp[:, h, :], r=[kpk], w=[kt])
                lb.ACT(qkT[:], pstb.rearrange("p (a b) -> p a b", a=8), AF.Copy, r=[kt], w=["qkT"])
                psS, kS = lb.psum()
                for h in range(4):
                    lb.MM(psS[:, h * 128:(h + 1) * 128], qkT[:, 4 + h, :], qkT[:, h, :], r=["qkT"], w=[kS])
                lb.TT(ST[:], psS[:].rearrange("p (h t) -> p h t", h=4),
                      lb.trif[:].unsqueeze(1).to_broadcast([128, 4, 128]), ALU.mult, r=[kS, "trif"], w=["ST"])
                for half in range(2):
                    pso, ko = lb.psum()
                    for kc in range(8):
                        lb.MM(pso[:, :], hm[:, kc, :], win[:, kc, co + half * 512:co + (half + 1) * 512],
                              start=(kc == 0), stop=(kc == 7), r=[hmk, wink[kc]], w=[ko])
                    lb.ACT(og[:, half * 512:(half + 1) * 512], pso[:, :], AF.Sigmoid, r=[ko], w=["og"])
                lb.TT(og[:], og[:], nwrep[:], ALU.mult, r=["og", "nwrep"], w=["og"])
                for h in range(4):
                    psO, kO = lb.psum()
                    lb.MM(psO[:, 0:257], ST[:, h, :], ve[:, h, :], start=True, stop=False, r=["ST", vek], w=[kO])
                    lb.MM(psO[:, 0:257], qkT[:, h, :], Cb[:, h, :], start=False, stop=True,
                          r=["qkT", ("Cb", h)], w=[kO])
                    lb.CP(hd[:, h:h + 1], psO[:, 256:257], r=[kO], w=["hd"])
                    lb.STT(hd[:, h:h + 1], hd[:, h:h + 1], -1.0, hd[:, h:h + 1], ALU.mult, ALU.max, r=["hd"], w=["hd"])
                    lb.P.op("dve", lambda e, o=hd[:, h:h + 1]: e.tensor_scalar_max(out=o, in0=o, scalar1=1.0), ["hd"], ["hd"])
                    lb.RECIP(hd[:, 4 + h:5 + h], hd[:, h:h + 1], r=["hd"], w=["hd"])
                    hh_h = hh[:, h * 256:(h + 1) * 256]
                    lb.ACT(hh_h, psO[:, 0:256], AF.Copy, r=[kO, "hd"], w=[("hh", h)], scale=hd[:, 4 + h:5 + h])
                    lb.STT(lb.junk[:, 0:256], hh_h, 1.0, hh_h, ALU.mult, ALU.mult, r=[("hh", h)], w=["junk", "hd"],
                           accum_out=hd[:, 8 + h:9 + h])
                lb.ACT(hd[:, 12:16], hd[:, 8:12], AF.Sqrt, r=["hd"], w=["hd"], scale=1.0 / 256, bias=EPS)
                lb.RECIP(hd[:, 12:16], hd[:, 12:16], r=["hd"], w=["hd"])
                for h in range(4):
                    sl = slice(h * 256, (h + 1) * 256)
                    lb.STT(hsb[:, sl], hh[:, sl], hd[:, 12 + h:13 + h], og[:, sl], ALU.mult, ALU.mult,
                           r=[("hh", h), "hd", "og"], w=["hsb"])
                pst2, kt2 = lb.psum()
                pst2b = pst2[:].bitcast(BF16)
                for kc in range(8):
                    lb.TR(pst2b[:, kc * 128:(kc + 1) * 128], hsb[:, kc * 128:(kc + 1) * 128], r=["hsb"], w=[kt2])
                lb.ACT(hsT[:], pst2b.rearrange("p (a b) -> p a b", a=8), AF.Copy, r=[kt2], w=["hsT"])
                for half in range(2):
                    psy, ky = lb.psum()
                    for kc in range(8):
                        lb.MM(psy[:, :], hsT[:, kc, :], wout[:, kc, half * 512:(half + 1) * 512], start=(kc == 0),
                              stop=(kc == 7), r=["hsT", "wout"], w=[ky])
                    lb.epi_half(tt, half, psy, ky)
                lb.epi_final(tt)
            for h in range(4):
                psU, kU = lb.psum()
                lb.MM(psU[:, 0:257], kp[:, h, :], ve[:, h, :], r=[kpk, vek], w=[kU])
                lb.TSM(C[:, h, :], C[:, h, :], gt[:, 20 + h:21 + h], r=[("C", h), gtk], w=[("C", h)])
                lb.STT(C[:, h, :], psU[:, 0:257], gt[:, 20 + h:21 + h], C[:, h, :], ALU.mult, ALU.add,
                       r=[kU, gtk, ("C", h)], w=[("C", h)])
                if full:
                    lb.CP(Cb[:, h, :], C[:, h, :], r=[("C", h)], w=[("Cb", h)])
        if full:
            lb.store_xg(g, x_out)
    if not full:
        o = lb.DMA(c_out, C[:].rearrange("p a b -> p (a b)"), r=[("C", h) for h in range(4)], w=["c_out_d"])
        lb.out_ops.append(o)
        o = lb.DMA(g_out, gacc[0:1, :], r=["gacc"], w=["g_out_d"])
        lb.out_ops.append(o)
        if lb.fused:
            lb.collect(c_out, lb.io["C_gath"], r=["c_out_d"], w=["c_gath_d"])
            lb.collect(g_out, lb.io["nG_gath"], r=["g_out_d"], w=["g_gath_d"])
    return lb.finish()


MB_GS = 2


def build_mb_tail(lb=None):
    lb = lb or LB()
    lb.consts()
    lb.xgroup(1, 128)
    lb.norm_scratch()
    lb.junkf = lb.sb("junkf", [128, 512])
    scr = lb.sb("modscr", [128, 4096])
    lb.mod(1.0, (scr[:], ["modscr"]))
    w_xbc = lb.inp("w_xbc", [D, 3072])
    tail_out = lb.out("tail", [3, 3072])
    wx = lb.sb("wx", [128, 8, 3072], BF16)
    lb.DMA(wx[:], w_xbc.rearrange("(kc p) n -> p kc n", p=128), w=["wx"], q="pool")
    hmT = lb.sb("hmT", [128, 8, 128], BF16)
    xb = lb.sb("xb", [128, 3072])
    lb.load_xg(0)
    lb.prenorm_T(0, hmT, "hmT", 0)
    for blk in range(6):
        ps, pk = lb.psum()
        for kc in range(8):
            lb.MM(ps[:, :], hmT[:, kc, :], wx[:, kc, blk * 512:(blk + 1) * 512], start=(kc == 0), stop=(kc == 7),
                  r=["hmT", "wx"], w=[pk])
        lb.ACT(xb[:, blk * 512:(blk + 1) * 512], ps[:, :], AF.Copy, r=[pk], w=["xb"])
    o = lb.DMA(tail_out, xb[125:128, :], r=["xb"], w=["tail_d"])
    lb.out_ops.append(o)
    if lb.fused:
        lb.collect(tail_out, lb.io["tail_gath"], r=["tail_d"], w=["tail_gath_d"])
    return lb.finish()


def build_mb(full, lb=None):
    GS = MB_GS
    W = GS * 128
    lb = lb or LB()
    lb.consts()
    slow_in = lb.inp("slow", [128, 128])
    lb.xgroup(GS)
    lb.norm_scratch()
    lb.junkf = lb.sb("junkf", [128, 512])
    ncc = 24 if full else 20
    w_xbc = lb.inp("w_xbc", [D, ncc * 128])
    w_dt = lb.inp("w_dt", [D, 32])
    cw_in = lb.inp("conv_w", [128, 24 * 4])
    cb_in = lb.inp("conv_b", [128, 24])
    tail_in = None if lb.fused else lb.inp("tail", [128, 24 * 3])
    dtb_in = lb.inp("dt_bias", [1, 32])
    alog_in = lb.inp("A_log", [1, 32])
    wdt = lb.sb("wdt", [128, 8, 32], BF16)
    lb.DMA(wdt[:], w_dt.rearrange("(kc p) n -> p kc n", p=128), w=["wdt"], q="pool")
    cw = lb.sb("cw", [128, 24, 4])
    cb = lb.sb("cb", [128, 24])
    carry = lb.sb("carry", [128, 24, 3])
    lb.DMA(cw[:].rearrange("p a b -> p (a b)"), cw_in, w=["cw"])
    lb.DMA(cb[:], cb_in, w=["cb"])
    if not lb.fused:
        lb.DMA(carry[:].rearrange("p a b -> p (a b)"), tail_in, w=[("carry", cc) for cc in range(24)])
    dtb = lb.sb("dtb", [128, 32])
    expA = lb.sb("expA", [128, 32])
    lb.DMA(dtb[:], dtb_in[0, :].partition_broadcast(128), w=["dtb"])
    lb.DMA(expA[:], alog_in[0, :].partition_broadcast(128), w=["expA"])
    lb.ACT(expA[:], expA[:], AF.Exp, r=["expA"], w=["expA"])
    hmT = lb.sb("hmT", [128, 8, W], BF16)
    xprer = Ring(lb, "xpre", [128, W + 3], F32, 2)
    accr = Ring(lb, "acc", [128, W], F32, 2)
    xF = lb.sb("xF", [128, 16, 512], BF16)
    BT = lb.sb("BT", [128, 4, W], BF16)
    dtsr = Ring(lb, "dts", [128, 192], F32, 2)
    xwr = Ring(lb, "xw", [128, 32, 64], BF16, 1)
    Btmr = Ring(lb, "Btm", [128, 4, 128], BF16, 2)
    S = lb.sb("S", [128, 4, 512])
    nacc = lb.sb("nacc", [128, 32])
    lb.MS(nacc[:], 0.0, w=["nacc"])
    lb.MS(S[:], 0.0, w=[("S", q) for q in range(4)])
    xFk = [("xF", cc) for cc in range(16)]
    xf_scr = (xF[:].rearrange("p a b -> p (a b)").bitcast(F32), xFk)
    if lb.fused:
        tsel_in = lb.inp("tail_sel", [24, 3])
        tsel = lb.sb("tsel", [24, 3])
        lb.DMA(tsel[:], tsel_in, w=["tsel"])
        tl = xf_scr[0][0:24, 0:3072]
        lb.DMA(tl, lb.io["tails_all"], w=xFk)
        for cc in range(24):
            ps, pk = lb.psum()
            lb.MM(ps[:, 0:3], tl[:, cc * 128:(cc + 1) * 128], tsel[:], r=xFk + ["tsel"], w=[pk])
            lb.CP(carry[:, cc, :], ps[:, 0:3], r=[pk], w=[("carry", cc)])
    if full:
        slow = lb.sb("slow_sb", [128, 128])
        lb.DMA(slow[:], slow_in, w=["slow"])
        wxr = Ring(lb, "wxb", [128, 8, 256], BF16, 2)
        wzr = Ring(lb, "wzb", [128, 8, 256], BF16, 2)
        lb.rawr = Ring(lb, "raw", [128, 512], F32, 2)
        lb.yb = lb.sb("yb", [128, GS, D])
        lb.es2 = lb.sb("es2", [128, 4, 4])
        x_out = lb.out("x_out", [TPC, D])
        lb.mod(1.0, xf_scr)
        w_z = lb.inp("w_z", [D, 2048])
        w_out = lb.inp("w_out", [2048, D])
        normw = lb.inp("norm_w", [1, 2048])
        d_in = lb.inp("D_skip", [1, 32])
        wout = lb.sb("wout", [128, 16, D], BF16)
        lb.DMA(wout[:], w_out.rearrange("(kc p) n -> p kc n", p=128), w=["wout"], q="pool")
        nwrep = lb.sb("nwrep", [128, 2048])
        lb.DMA(nwrep[:], normw[0, :].partition_broadcast(128), w=["nwrep"])
        Dr = lb.sb("Dr", [128, 32])
        lb.DMA(Dr[:], d_in[0, :].partition_broadcast(128), w=["Dr"])
        CT = lb.sb("CT", [128, 4, W], BF16)
        zs = lb.sb("zs", [128, GS, 2048], BF16)
        Sb = lb.sb("Sb", [128, 4, 512], BF16)
        Rr = lb.sb("Rr", [128, 8, 128])
        E = lb.sb("E", [128, 8, 128], BF16)
        WT = lb.sb("WT", [128, 8, 128], BF16)
        GTt = lb.sb("GT", [128, 4, 128], BF16)
        xs_ = lb.sb("xs_", [128, 32, 64], BF16)
        xD = lb.sb("xD", [128, 32, 64], BF16)
        yt = lb.sb("yt", [128, 4, 512])
        nd = lb.sb("nd", [128, 8])
        ynb = lb.sb("ynb", [128, 2048], BF16)
        ynT = lb.sb("ynT", [128, 16, 128], BF16)
        Sall = lb.inp("S_all", [NCORES, 128, 2048])
        nAall = lb.inp("nA_all", [NCORES, 32])
        pm = lb.inp("pm", [NCORES, NCORES])
        selr = lb.inp("sel_rep32", [1, 256])
        nAs = lb.sb("nAs", [8, 32])
        pms = lb.sb("pms", [8, 8])
        Rm = lb.sb("Rm", [8, 8, 32])
        wrep = lb.sb("wrep", [128, 256])
        sel = lb.sb("sel", [128, 256])
        lb.DMA(nAs[:], nAall, w=["nAs"])
        lb.DMA(pms[:], pm, w=["pms"])
        lb.DMA(sel[:], selr[0, :].partition_broadcast(128), w=["sel"])
        lb.TT(Rm[:], pms[:].unsqueeze(2).to_broadcast([8, 8, 32]), nAs[:].unsqueeze(1).to_broadcast([8, 8, 32]),
              ALU.mult, r=["pms", "nAs"], w=["Rm"])
        ps, pk = lb.psum()
        lb.MM(ps[:, 0:256], lb.onesf[0:8, :], Rm[:].rearrange("p a b -> p (a b)"), r=["onesf", "Rm"], w=[pk])
        lb.ACT(wrep[:], ps[:, 0:256], AF.Exp, r=[pk], w=["wrep"], scale=-1.0)
        lb.TT(wrep[:], wrep[:], sel[:], ALU.mult, r=["wrep", "sel"], w=["wrep"])
        sl_scr = xf_scr[0]
        for i in range(NCORES):
            half = i % 2
            sb_ = sl_scr[:, half * 2048:(half + 1) * 2048]
            sbk = [("xF", cc) for cc in range(half * 8, (half + 1) * 8)]
            lb.DMA(sb_, Sall[i], w=sbk)
            for q in range(4):
                v = sb_[:, q * 512:(q + 1) * 512].rearrange("p (h d) -> p h d", h=8)
                lb.TT(v, v, wrep[:, i * 32 + q * 8:i * 32 + (q + 1) * 8].unsqueeze(2).to_broadcast([128, 8, 64]),
                      ALU.mult, r=sbk + ["wrep"], w=sbk)
                lb.TT(S[:, q, :], S[:, q, :], sb_[:, q * 512:(q + 1) * 512], ALU.add, r=sbk + [("S", q)],
                      w=[("S", q)])
        for q in range(4):
            lb.CP(Sb[:, q, :], S[:, q, :], r=[("S", q)], w=[("Sb", q)])
    else:
        wx = lb.sb("wx", [128, 8, ncc * 128], BF16)
        lb.DMA(wx[:], w_xbc.rearrange("(kc p) n -> p kc n", p=128), w=["wx"], q="pool")
        lb.mod(1.0, xf_scr)
        s_out = lb.out("S_loc", [128, 2048])
        a_out = lb.out("nA", [1, 32])

    for g in range(NT // GS):
        lb.load_xg(g)
        for tt in range(GS):
            lb.prenorm_T(tt, hmT, "hmT", tt)
        if full:
            for blk in range(8):
                wzb, wzk = wzr.next()
                lb.DMA(wzb[:], w_z[:, blk * 256:(blk + 1) * 256].rearrange("(kc p) n -> p kc n", p=128), w=[wzk],
                       q="pool")
                for tt in range(GS):
                    psz, kz = lb.psum()
                    for kc in range(8):
                        lb.MM(psz[:, 0:256], hmT[:, kc, tt * 128:(tt + 1) * 128], wzb[:, kc, :], start=(kc == 0),
                              stop=(kc == 7), r=["hmT", wzk], w=[kz])
                    lb.ACT(zs[:, tt, blk * 256:(blk + 1) * 256], psz[:, 0:256], AF.Silu, r=[kz], w=[("zs", tt)])
        for cc in range(ncc):
            if full:
                if cc % 2 == 0:
                    wxb, wxk = wxr.next()
                    lb.DMA(wxb[:], w_xbc[:, cc * 128:(cc + 2) * 128].rearrange("(kc p) n -> p kc n", p=128),
                           w=[wxk], q="pool")
                wsl = lambda kc, cc=cc, wxb=wxb: wxb[:, kc, (cc % 2) * 128:(cc % 2 + 1) * 128]
            else:
                wxk = "wx"
                wsl = lambda kc, cc=cc: wx[:, kc, cc * 128:(cc + 1) * 128]
            psc, kc_ = lb.psum()
            for kc in range(8):
                lb.MM(psc[:, 0:W], wsl(kc), hmT[:, kc, :], start=(kc == 0), stop=(kc == 7), r=[wxk, "hmT"], w=[kc_])
            xp, xpk = xprer.next()
            lb.CP(xp[:, 0:3], carry[:, cc, :], r=[("carry", cc)], w=[xpk])
            lb.ACT(xp[:, 3:W + 3], psc[:, 0:W], AF.Copy, r=[kc_], w=[xpk])
            lb.CP(carry[:, cc, :], xp[:, W:W + 3], r=[xpk], w=[("carry", cc)])
            ac, ack = accr.next()
            lb.TSM(ac[:], xp[:, 0:W], cw[:, cc, 0:1], r=[xpk, "cw"], w=[ack])
            for j in range(1, 4):
                lb.STT(ac[:], xp[:, j:j + W], cw[:, cc, j:j + 1], ac[:], ALU.mult, ALU.add, r=[xpk, "cw", ack],
                       w=[ack])
            if cc < 16:
                dst, dk = xF[:, cc, 0:W], ("xF", cc)
            elif cc < 20:
                dst, dk = BT[:, cc - 16, :], ("BT", cc - 16)
            else:
                dst, dk = CT[:, cc - 20, :], ("CT", cc - 20)
            lb.ACT(dst, ac[:], AF.Silu, r=[ack, "cb"], w=[dk], bias=cb[:, cc:cc + 1])
        for tt in range(GS):
            tc_ = slice(tt * 128, (tt + 1) * 128)
            psd, kd = lb.psum()
            for kc in range(8):
                lb.MM(psd[:, 0:32], hmT[:, kc, tc_], wdt[:, kc, :], start=(kc == 0), stop=(kc == 7),
                      r=["hmT", "wdt"], w=[kd])
            dt_, dtk = dtsr.next()
            lb.TT(dt_[:, 0:32], psd[:, 0:32], dtb[:], ALU.add, r=[kd, "dtb"], w=[dtk])
            lb.ACT(dt_[:, 0:32], dt_[:, 0:32], AF.Exp, r=[dtk], w=[dtk])
            lb.ACT(dt_[:, 0:32], dt_[:, 0:32], AF.Ln, r=[dtk], w=[dtk], bias=1.0)
            lb.TT(dt_[:, 32:64], dt_[:, 0:32], expA[:], ALU.mult, r=[dtk, "expA"], w=[dtk])
            psa, ka = lb.psum()
            lb.MM(psa[:, 0:32], lb.trif[:], dt_[:, 32:64], r=["trif", dtk], w=[ka])
            lb.MM(psa[:, 32:64], lb.onesf[:], dt_[:, 32:64], r=["onesf", dtk], w=[ka])
            lb.CP(dt_[:, 64:128], psa[:, 0:64], r=[ka], w=[dtk])
            lb.TT(dt_[:, 128:160], dt_[:, 96:128], dt_[:, 64:96], ALU.subtract, r=[dtk], w=[dtk])
            lb.ACT(dt_[:, 128:160], dt_[:, 128:160], AF.Exp, r=[dtk], w=[dtk], scale=-1.0)
            lb.TT(dt_[:, 128:160], dt_[:, 128:160], dt_[:, 0:32], ALU.mult, r=[dtk], w=[dtk])
            lb.TT(nacc[:], nacc[:], dt_[:, 96:128], ALU.add, r=["nacc", dtk], w=["nacc"])
            lb.ACT(dt_[:, 96:128], dt_[:, 96:128], AF.Exp, r=[dtk], w=[dtk], scale=-1.0)
            lb.ACT(dt_[:, 160:192], dt_[:, 64:96], AF.Exp, r=[dtk], w=[dtk], scale=-1.0)
            xw, xwk = xwr.next()
            for half in range(2):
                pst, kt = lb.psum()
                pstb = pst[:].bitcast(BF16)
                for q in range(8):
                    cc = half * 8 + q
                    lb.TR(pstb[:, q * 128:(q + 1) * 128], xF[:, cc, tc_], r=[("xF", cc)], w=[kt])
                hs_ = slice(half * 16, (half + 1) * 16)
                pv = pstb.rearrange("p (h d) -> p h d", h=16)
                lb.TT(xw[:, hs_, :], pv, dt_[:, 128 + half * 16:128 + (half + 1) * 16].unsqueeze(2).to_broadcast(
                    [128, 16, 64]), ALU.mult, r=[kt, dtk], w=[xwk])
                if full:
                    lb.TT(xs_[:, hs_, :], pv, dt_[:, half * 16:(half + 1) * 16].unsqueeze(2).to_broadcast(
                        [128, 16, 64]), ALU.mult, r=[kt, dtk], w=["xs_"])
                    lb.TT(xD[:, hs_, :], pv, Dr[:, hs_].unsqueeze(2).to_broadcast([128, 16, 64]), ALU.mult,
                          r=[kt, "Dr"], w=["xD"])
            pstB, ktB = lb.psum()
            pstBb = pstB[:].bitcast(BF16)
            for q in range(4):
                lb.TR(pstBb[:, q * 128:(q + 1) * 128], BT[:, q, tc_], r=[("BT", q)], w=[ktB])
            Btm, Btmk = Btmr.next()
            lb.ACT(Btm[:], pstBb[:, 0:512].rearrange("p (a b) -> p a b", a=4), AF.Copy, r=[ktB], w=[Btmk])
            if full:
                psG, kG = lb.psum()
                for q in range(4):
                    lb.MM(psG[:, q * 128:(q + 1) * 128], BT[:, q, tc_], CT[:, q, tc_], r=[("BT", q), ("CT", q)],
                          w=[kG])
                lb.TT(GTt[:], psG[:].rearrange("p (a b) -> p a b", a=4),
                      lb.trif[:].unsqueeze(1).to_broadcast([128, 4, 128]), ALU.mult, r=[kG, "trif"], w=["GT"])
                for q in range(4):
                    h8 = slice(q * 8, (q + 1) * 8)
                    lb.TT(Rr[:], dt_[:, 32 + q * 8:32 + (q + 1) * 8].unsqueeze(2).to_broadcast([128, 8, 128]),
                          lb.trif[:].unsqueeze(1).to_broadcast([128, 8, 128]), ALU.mult, r=[dtk, "trif"], w=["Rr"],
                          eng="pool")
                    for half in range(2):
                        psE, kE = lb.psum()
                        lb.MM(psE[:, :], slow[:], Rr[:, half * 4:(half + 1) * 4, :].rearrange("p a b -> p (a b)"),
                              r=["slow", "Rr"], w=[kE])
                        lb.ACT(E[:, half * 4:(half + 1) * 4, :], psE[:].rearrange("p (a b) -> p a b", a=4), AF.Exp,
                               r=[kE], w=["E"], scale=-1.0)
                    lb.TT(WT[:], E[:], GTt[:, q, :].unsqueeze(1).to_broadcast([128, 8, 128]), ALU.mult,
                          r=["E", "GT"], w=["WT"])
                    psY, kY = lb.psum()
                    for hh in range(8):
                        h = q * 8 + hh
                        reg = psY[:, hh * 64:(hh + 1) * 64]
                        lb.MM(reg, lb.identb[:], xD[:, h, :], start=True, stop=False, r=["identb", "xD"], w=[kY])
                        lb.MM(reg, WT[:, hh, :], xs_[:, h, :], start=False, stop=True, r=["WT", "xs_"], w=[kY])
                    psI, kI = lb.psum()
                    lb.MM(psI[:, :], CT[:, q, tc_], Sb[:, q, :], r=[("CT", q), ("Sb", q)], w=[kI])
                    yv = yt[:, q, :].rearrange("p (h d) -> p h d", h=8)
                    lb.TT(yv, psI[:].rearrange("p (h d) -> p h d", h=8),
                          dt_[:, 160 + q * 8:160 + (q + 1) * 8].unsqueeze(2).to_broadcast([128, 8, 64]), ALU.mult,
                          r=[kI, dtk], w=[("yt", q)])
                    lb.TT(yt[:, q, :], yt[:, q, :], psY[:, :], ALU.add, r=[("yt", q), kY], w=[("yt", q)])
                    lb.TT(yt[:, q, :], yt[:, q, :], zs[:, tt, q * 512:(q + 1) * 512], ALU.mult,
                          r=[("yt", q), ("zs", tt)], w=[("yt", q)])
                    lb.STT(lb.junk[:, 0:512], yt[:, q, :], 1.0, yt[:, q, :], ALU.mult, ALU.mult, r=[("yt", q)],
                           w=["junk", "nd"], accum_out=nd[:, q:q + 1])
                lb.ACT(nd[:, 4:8], nd[:, 0:4], AF.Sqrt, r=["nd"], w=["nd"], scale=1.0 / 512, bias=EPS)
                lb.RECIP(nd[:, 4:8], nd[:, 4:8], r=["nd"], w=["nd"])
                for q in range(4):
                    sl = slice(q * 512, (q + 1) * 512)
                    lb.STT(ynb[:, sl], yt[:, q, :], nd[:, 4 + q:5 + q], nwrep[:, sl], ALU.mult, ALU.mult,
                           r=[("yt", q), "nd", "nwrep"], w=["ynb"])
                for half in range(2):
                    pst2, kt2 = lb.psum()
                    pst2b = pst2[:].bitcast(BF16)
                    for q8 in range(8):
                        kc = half * 8 + q8
                        lb.TR(pst2b[:, q8 * 128:(q8 + 1) * 128], ynb[:, kc * 128:(kc + 1) * 128], r=["ynb"], w=[kt2])
                    lb.ACT(ynT[:, half * 8:(half + 1) * 8, :], pst2b.rearrange("p (a b) -> p a b", a=8), AF.Copy,
                           r=[kt2], w=["ynT"])
                for half in range(2):
                    psy, ky = lb.psum()
                    for kc in range(16):
                        lb.MM(psy[:, :], ynT[:, kc, :], wout[:, kc, half * 512:(half + 1) * 512], start=(kc == 0),
                              stop=(kc == 15), r=["ynT", "wout"], w=[ky])
                    lb.epi_half(tt, half, psy, ky)
                lb.epi_final(tt)
            for q in range(4):
                psU, kU = lb.psum()
                lb.MM(psU[:, :], Btm[:, q, :], xw[:, q * 8:(q + 1) * 8, :].rearrange("p h d -> p (h d)"),
                      r=[Btmk, xwk], w=[kU])
                Sg = S[:, q, :].rearrange("p (h d) -> p h d", h=8)
                lb.TT(Sg, Sg, dt_[:, 96 + q * 8:96 + (q + 1) * 8].unsqueeze(2).to_broadcast([128, 8, 64]), ALU.mult,
                      r=[("S", q), dtk], w=[("S", q)])
                lb.TT(S[:, q, :], S[:, q, :], psU[:, :], ALU.add, r=[("S", q), kU], w=[("S", q)])
                if full:
                    lb.CP(Sb[:, q, :], S[:, q, :], r=[("S", q)], w=[("Sb", q)])
        if full:
            lb.store_xg(g, x_out)
    if not full:
        o = lb.DMA(s_out, S[:].rearrange("p a b -> p (a b)"), r=[("S", q) for q in range(4)], w=["s_out_d"])
        lb.out_ops.append(o)
        o = lb.DMA(a_out, nacc[0:1, :], r=["nacc"], w=["a_out_d"])
        lb.out_ops.append(o)
        if lb.fused:
            lb.collect(s_out, lb.io["S_gath"], r=["s_out_d"], w=["s_gath_d"])
            lb.collect(a_out, lb.io["nA_gath"], r=["a_out_d"], w=["a_gath_d"])
    return lb.finish()


USE_MODPHASE = True
MODN = 4 * 9 * D // NCORES


def build_modphase(lb):
    lb.consts()
    modw = lb.inp("modw", [MODN // 512, D, 512])
    modb = lb.inp("modb", [1, MODN])
    c_in = lb.inp("c_l", [128, 8])
    cs = lb.sb("cs", [128, 8])
    res = lb.sb("modres", [128, MODN])
    wr = Ring(lb, "modwb", [128, 8, 512], F32, 2)
    ar = Ring(lb, "modacc", [128, 512], F32, 2)
    lb.DMA(cs[:], c_in, w=["cs"])
    lb.ACT(cs[:], cs[:], AF.Silu, r=["cs"], w=["cs"])
    lb.DMA(res[:], modb[0, :].partition_broadcast(128), w=["res"])
    for b in range(MODN // 512):
        wb, wk = wr.next()
        lb.DMA(wb[:], modw[b].rearrange("(kc p) n -> p kc n", p=128), w=[wk])
        acc, ak = ar.next()
        lb.TSM(acc[:], wb[:, 0, :], cs[:, 0:1], r=[wk, "cs"], w=[ak])
        for kc in range(1, 8):
            lb.STT(acc[:], wb[:, kc, :], cs[:, kc:kc + 1], acc[:], ALU.mult, ALU.add, r=[wk, "cs", ak], w=[ak])
        ps, pk = lb.psum()
        lb.MM(ps[:, :], lb.onesf[:], acc[:], r=["onesf", ak], w=[pk])
        sl = res[:, b * 512:(b + 1) * 512]
        lb.TT(sl, ps[:, :], sl, ALU.add, r=[pk, "res"], w=["res"])
    o = lb.DMA(lb.io["mod_loc"], res[0:1, :], r=["res"], w=["mod_loc_d"])
    lb.out_ops.append(o)
    lb.collect(lb.io["mod_loc"], lb.io["modall"], r=["mod_loc_d"], w=["modall_d"])
    return lb.finish()


def build_fused(layers=(0, 1, 2, 3)):
    Fz = Fused()
    nc = Fz.nc
    x_in = Fz.dram_in("x", [TPC, D])
    x_out = nc.dram_tensor("x_out", [TPC, D], F32, kind="ExternalOutput").ap()
    xa = nc.dram_tensor("xa", [TPC, D], F32).ap()

    def internal(name, shape):
        return nc.dram_tensor(name, list(shape), F32).ap()

    if USE_MODPHASE:
        mod_loc = internal("mod_loc", [1, MODN])
        modall = internal("modall", [NCORES, MODN])
        build_modphase(LB(Fz, "M_", {"mod_loc": mod_loc, "modall": modall}))

    def mio(i, s_, d):
        d = dict(d)
        if USE_MODPHASE and (i % 2 == 0 or s_ != 1):
            d["modall"] = modall
            d["mod_base"] = (i * 9 + s_ * 3) * D
        return d

    for n, i in enumerate(layers):
        src = x_in if n == 0 else xa
        build_ffn(LB(Fz, "L%df0_" % i, mio(i, 0, {"x": src, "x_out": xa})))
        if i % 2 == 0:
            cl = internal("L%d_cl" % i, [128, 1028])
            ng = internal("L%d_ng" % i, [1, 4])
            cg = internal("L%d_cg" % i, [NCORES * 128, 1028])
            ngg = internal("L%d_ngg" % i, [NCORES, 4])
            build_ml(False, LB(Fz, "L%dp1_" % i, mio(i, 1, {"x": xa, "C_loc": cl, "nG": ng, "C_gath": cg,
                                                             "nG_gath": ngg})))
            build_ml(True, LB(Fz, "L%dp2_" % i, mio(i, 1, {"x": xa, "x_out": xa, "nG_all": ngg,
                                                            "C_all": cg.rearrange("(r p) c -> r p c", p=128)})))
        else:
            tb = internal("L%d_tb" % i, [3, 3072])
            tg = internal("L%d_tg" % i, [NCORES * 3, 3072])
            sl = internal("L%d_sl" % i, [128, 2048])
            na = internal("L%d_na" % i, [1, 32])
            sg = internal("L%d_sg" % i, [NCORES * 128, 2048])
            nag = internal("L%d_nag" % i, [NCORES, 32])
            build_mb_tail(LB(Fz, "L%dpt_" % i, mio(i, 1, {"x": xa[TPC - 128:TPC, :], "tail": tb, "tail_gath": tg})))
            build_mb(False, LB(Fz, "L%dp1_" % i, mio(i, 1, {"x": xa, "tails_all": tg, "S_loc": sl, "nA": na,
                                                             "S_gath": sg, "nA_gath": nag})))
            build_mb(True, LB(Fz, "L%dp2_" % i, mio(i, 1, {"x": xa, "x_out": xa, "tails_all": tg, "nA_all": nag,
                                                            "S_all": sg.rearrange("(r p) c -> r p c", p=128)})))
        dst = x_out if n == len(layers) - 1 else xa
        build_ffn(LB(Fz, "L%df1_" % i, mio(i, 2, {"x": xa, "x_out": dst})))
    names = list(Fz.in_names)
    return Fz.close(), names


def fused_inputs(inp, layers=(0, 1, 2, 3)):
    sh = dict(_consts())
    sh["slow"] = np.tril(np.ones((128, 128), np.float32), -1)
    sh["c_l"] = _f32(inp["c"][0].reshape(8, 128).T)

    def put(tag, d):
        for k, v in d.items():
            if k not in SHARED_INPUTS:
                sh[tag + k] = v

    for i in layers:
        j = i // 2
        for f in range(2):
            w1 = inp["ffn_w1"][i, f].reshape(8, 128, NFC, 128)
            w3 = inp["ffn_w3"][i, f].reshape(8, 128, NFC, 128)
            w13 = np.stack([w1, w3], 0)
            d = _mod_inputs(inp, i, 0 if f == 0 else 2)
            d["w13"] = _f32(w13.transpose(3, 2, 0, 1, 4).reshape(NFC, 128, 2048))
            d["w2"] = _f32(inp["ffn_w2"][i, f])
            put("L%df%d_" % (i, f), d)
        m = _mod_inputs(inp, i, 1)
        if i % 2 == 0:
            w_in = inp["ml_w_in"][j]
            d = dict(m)
            d["b_gate"] = _f32(inp["ml_b_gate"][j].reshape(1, 8))
            d1 = dict(d)
            d1["w_in"] = _f32(np.concatenate([w_in[:, 512:2048], w_in[:, 3072:3080]], axis=1))
            put("L%dp1_" % i, d1)
            d2 = dict(d)
            d2["w_in"] = _f32(w_in)
            d2["w_out"] = _f32(inp["ml_w_out"][j])
            d2["norm_w"] = _f32(inp["ml_norm_w"][j].reshape(1, D))
            put("L%dp2_" % i, d2)
        else:
            w_in = inp["mb_w_in"][j]
            dt_ = dict(m)
            dt_["w_xbc"] = _f32(w_in[:, 2048:5120])
            put("L%dpt_" % i, dt_)
            d = dict(m)
            d["w_dt"] = _f32(w_in[:, 5120:5152])
            d["conv_w"] = _f32(inp["mb_conv_w"][j].T.reshape(24, 128, 4).transpose(1, 0, 2).reshape(128, 96))
            d["conv_b"] = _f32(inp["mb_conv_b"][j].reshape(24, 128).T)
            d["dt_bias"] = _f32(inp["mb_dt_bias"][j].reshape(1, 32))
            d["A_log"] = _f32(inp["mb_A_log"][j].reshape(1, 32))
            d1 = dict(d)
            d1["w_xbc"] = _f32(w_in[:, 2048:4608])
            put("L%dp1_" % i, d1)
            d2 = dict(d)
            d2["w_xbc"] = _f32(w_in[:, 2048:5120])
            d2["w_z"] = _f32(w_in[:, 0:2048])
            d2["w_out"] = _f32(inp["mb_w_out"][j])
            d2["norm_w"] = _f32(inp["mb_norm_w"][j].reshape(1, 2048))
            d2["D_skip"] = _f32(inp["mb_D"][j].reshape(1, 32))
            put("L%dp2_" % i, d2)
    return sh


def core_inputs(c, inp):
    pm, sel = _core_masks(c)
    wall = [inp["ada_w"][i][:, c * MODN - i * 9 * D:(c + 1) * MODN - i * 9 * D] for i in range(4)
            if i * 9 * D <= c * MODN < (i + 1) * 9 * D]
    assert len(wall) == 1 and wall[0].shape[1] == MODN
    modw = _f32(wall[0].reshape(D, MODN // 512, 512).transpose(1, 0, 2))
    ball = np.concatenate([inp["ada_b"][i] for i in range(4)])
    modb = _f32(ball[c * MODN:(c + 1) * MODN].reshape(1, MODN))
    tsel = np.zeros((NCORES * 3, 3), np.float32)
    if c > 0:
        for r in range(3):
            tsel[(c - 1) * 3 + r, r] = 1.0
    return {"pm": pm, "sel_rep4": _f32(np.repeat(sel, 4).reshape(1, 32)),
            "sel_rep32": _f32(np.repeat(sel, 32).reshape(1, 256)), "tail_sel": tsel,
            "M_modw": modw, "M_modb": modb}


_FUSED = {}


def run_fused(inp, xs, layers=(0, 1, 2, 3)):
    key = tuple(layers)
    if key not in _FUSED:
        _FUSED[key] = build_fused(layers)
    nc, names = _FUSED[key]
    sh = fused_inputs(inp, layers)
    maps = []
    for c in range(NCORES):
        m = dict(sh)
        m.update(core_inputs(c, inp))
        m["x"] = xs[c]
        maps.append({k: m[k] for k in names})
    res = _run(nc, maps)
    return [r["x_out"] for r in res]


_PROGS = {}


def _prog(name, builder):
    if name not in _PROGS:
        _PROGS[name] = builder()
    return _PROGS[name]


def _run(nc, in_maps):
    res = run_bass_kernel_spmd(nc, in_maps, core_ids=list(range(NCORES)))
    return res.results


def _f32(a):
    return np.ascontiguousarray(np.asarray(a, dtype=np.float32))


def _consts():
    ident = np.eye(128, dtype=np.float32)
    tri = np.triu(np.ones((128, 128), dtype=np.float32))
    return {"ident": ident, "tri": tri}


def _mod_inputs(inp, i, s):
    return {
        "c_l": _f32(inp["c"][0].reshape(8, 128).T),
        "ada_w": _f32(inp["ada_w"][i][:, s * 3 * D:(s + 1) * 3 * D]),
        "ada_b": _f32(inp["ada_b"][i][s * 3 * D:(s + 1) * 3 * D].reshape(1, 3 * D)),
        "g_pre": _f32(inp["norm_pre"][i, s].reshape(1, D)),
        "g_post": _f32(inp["norm_post"][i, s].reshape(1, D)),
    }


def run_ffn(inp, xs, i, f):
    nc = _prog("ffn", build_ffn)
    w1 = inp["ffn_w1"][i, f].reshape(8, 128, NFC, 128)
    w3 = inp["ffn_w3"][i, f].reshape(8, 128, NFC, 128)
    w13 = np.stack([w1, w3], 0)
    w13 = _f32(w13.transpose(3, 2, 0, 1, 4).reshape(NFC, 128, 2048))
    base = dict(_consts())
    base.update(_mod_inputs(inp, i, 0 if f == 0 else 2))
    base["w13"] = w13
    base["w2"] = _f32(inp["ffn_w2"][i, f])
    maps = []
    for c in range(NCORES):
        m = dict(base)
        m["x"] = xs[c]
        maps.append(m)
    res = _run(nc, maps)
    return [r["x_out"] for r in res]


def _core_masks(j):
    pm = np.zeros((NCORES, NCORES), np.float32)
    for l in range(NCORES):
        for i in range(NCORES):
            if i < l < j:
                pm[l, i] = 1.0
    sel = np.array([1.0 if i < j else 0.0 for i in range(NCORES)], np.float32)
    return pm, sel


def run_ml(inp, xs, i):
    j = i // 2
    w_in = inp["ml_w_in"][j]
    base = dict(_consts())
    base.update(_mod_inputs(inp, i, 1))
    base["b_gate"] = _f32(inp["ml_b_gate"][j].reshape(1, 8))
    nc1 = _prog("ml1", lambda: build_ml(False))
    b1 = dict(base)
    b1["w_in"] = _f32(np.concatenate([w_in[:, 512:2048], w_in[:, 3072:3080]], axis=1))
    maps = []
    for c in range(NCORES):
        m = dict(b1)
        m["x"] = xs[c]
        maps.append(m)
    r1 = _run(nc1, maps)
    C_all = _f32(np.stack([r["C_loc"] for r in r1], 0))
    nG_all = _f32(np.concatenate([r["nG"] for r in r1], 0))
    nc2 = _prog("ml2", lambda: build_ml(True))
    b2 = dict(base)
    b2["w_in"] = _f32(w_in)
    b2["w_out"] = _f32(inp["ml_w_out"][j])
    b2["norm_w"] = _f32(inp["ml_norm_w"][j].reshape(1, D))
    b2["C_all"] = C_all
    b2["nG_all"] = nG_all
    maps = []
    for c in range(NCORES):
        m = dict(b2)
        m["x"] = xs[c]
        pm, sel = _core_masks(c)
        m["pm"] = pm
        m["sel_rep4"] = _f32(np.repeat(sel, 4).reshape(1, 32))
        maps.append(m)
    r2 = _run(nc2, maps)
    return [r["x_out"] for r in r2]


def run_mb(inp, xs, i):
    j = i // 2
    w_in = inp["mb_w_in"][j]
    base = dict(_consts())
    base["slow"] = np.tril(np.ones((128, 128), np.float32), -1)
    base.update(_mod_inputs(inp, i, 1))
    nct = _prog("mbt", build_mb_tail)
    bt = dict(_consts())
    bt.update(_mod_inputs(inp, i, 1))
    bt["w_xbc"] = _f32(w_in[:, 2048:5120])
    maps = []
    for c in range(NCORES):
        m = dict(bt)
        m["x"] = np.ascontiguousarray(xs[c][TPC - 128:TPC])
        maps.append(m)
    rt = _run(nct, maps)
    tails = [np.zeros((3, 3072), np.float32)] + [rt[c]["tail"] for c in range(NCORES - 1)]
    tails = [_f32(t.T.reshape(24, 128, 3).transpose(1, 0, 2).reshape(128, 72)) for t in tails]
    base["w_dt"] = _f32(w_in[:, 5120:5152])
    base["conv_w"] = _f32(inp["mb_conv_w"][j].T.reshape(24, 128, 4).transpose(1, 0, 2).reshape(128, 96))
    base["conv_b"] = _f32(inp["mb_conv_b"][j].reshape(24, 128).T)
    base["dt_bias"] = _f32(inp["mb_dt_bias"][j].reshape(1, 32))
    base["A_log"] = _f32(inp["mb_A_log"][j].reshape(1, 32))
    nc1 = _prog("mb1", lambda: build_mb(False))
    b1 = dict(base)
    b1["w_xbc"] = _f32(w_in[:, 2048:4608])
    maps = []
    for c in range(NCORES):
        m = dict(b1)
        m["x"] = xs[c]
        m["tail"] = tails[c]
        maps.append(m)
    r1 = _run(nc1, maps)
    S_all = _f32(np.stack([r["S_loc"] for r in r1], 0))
    nA_all = _f32(np.concatenate([r["nA"] for r in r1], 0))
    nc2 = _prog("mb2", lambda: build_mb(True))
    b2 = dict(base)
    b2["w_xbc"] = _f32(w_in[:, 2048:5120])
    b2["w_z"] = _f32(w_in[:, 0:2048])
    b2["w_out"] = _f32(inp["mb_w_out"][j])
    b2["norm_w"] = _f32(inp["mb_norm_w"][j].reshape(1, 2048))
    b2["D_skip"] = _f32(inp["mb_D"][j].reshape(1, 32))
    b2["S_all"] = S_all
    b2["nA_all"] = nA_all
    maps = []
    for c in range(NCORES):
        m = dict(b2)
        m["x"] = xs[c]
        m["tail"] = tails[c]
        pm, sel = _core_masks(c)
        m["pm"] = pm
        m["sel_rep32"] = _f32(np.repeat(sel, 32).reshape(1, 256))
        maps.append(m)
    r2 = _run(nc2, maps)
    return [r["x_out"] for r in r2]


def run_mixer(inp, xs, i):
    if i % 2 == 0:
        return run_ml(inp, xs, i)
    return run_mb(inp, xs, i)


def kernel(**inputs):
    inp = {k: np.asarray(v) for k, v in inputs.items()}
    x = _f32(inp["x"][0])
    xs = [np.ascontiguousarray(x[c * TPC:(c + 1) * TPC]) for c in range(NCORES)]
    xs = run_fused(inp, xs)
    return np.concatenate(xs, 0).reshape(1, SEQ, D).astype(np.float32)
```
